# Optimizing a Trainium2 kernel written in Bass

```python
import math
import jax
import jax.numpy as jnp
from jax import lax
import numpy as np

D_MODEL = 1024
BATCH = 8
SEQ = 4096
DEPTH = 4

N_MIXERS = 4
RMS_EPS = 1e-6
ROPE_THETA = 500000.0
ROPE_FRACTION = 4

A_HEADS = 8
A_HEAD_DIM = D_MODEL // A_HEADS
MOBA_BLOCK = 256
MOBA_TOPK = 3
MOBA_QCHUNK = 16

B_HEADS = 8
B_HEAD_DIM = D_MODEL // (2 * B_HEADS)
B_QBLOCK = 128
SUBLN_EPS = 1e-5

C_WIDTH = D_MODEL
C_BLOCKS = 8
C_BLOCK_DIM = C_WIDTH // C_BLOCKS
C_CONV = 4
LRU_C = 8.0

D_HEADS = 4
D_QK_DIM = D_MODEL // D_HEADS
D_V_DIM = 2 * D_QK_DIM
RET_CHUNK = 128
RET_THETA = 10000.0
GN_EPS = 1e-5

D_FF = 2816
FFN_CONV = 3

kernel_name = "hybrid_moba_diffattn_rglru_retention_trunk"


def rmsnorm(x, g, eps=RMS_EPS):
    xf = x.astype(jnp.float32)
    y = xf * lax.rsqrt(jnp.mean(jnp.square(xf), axis=-1, keepdims=True) + eps)
    return (y * g.astype(jnp.float32)).astype(x.dtype)


def rope_tables(seq, rot_dim, theta):
    inv = theta ** (-jnp.arange(0, rot_dim, 2, dtype=jnp.float32) / rot_dim)
    ang = jnp.arange(seq, dtype=jnp.float32)[:, None] * inv[None, :]
    return jnp.cos(ang), jnp.sin(ang)


def apply_rotary(x, cos, sin):
    half = cos.shape[-1]
    rot = 2 * half
    xf = x.astype(jnp.float32)
    x1, x2, xp = xf[..., :half], xf[..., half:rot], xf[..., rot:]
    c, s = cos[None, :, None, :], sin[None, :, None, :]
    return jnp.concatenate([x1 * c - x2 * s, x2 * c + x1 * s, xp], axis=-1).astype(x.dtype)


def causal_dwconv(x, w, b):
    width = w.shape[0]
    y = lax.conv_general_dilated(
        x, w[:, None, :], window_strides=(1,), padding=[(width - 1, 0)],
        dimension_numbers=('NWC', 'WIO', 'NWC'), feature_group_count=x.shape[-1])
    return y + b


def moba_mixer(x, w_in, w_out, cos, sin):
    B, S, _ = x.shape
    H, Dh = A_HEADS, A_HEAD_DIM
    qkv = (x @ w_in).reshape(B, S, 3, H, Dh)
    q = apply_rotary(qkv[:, :, 0], cos, sin).transpose(0, 2, 1, 3)
    k = apply_rotary(qkv[:, :, 1], cos, sin).transpose(0, 2, 1, 3)
    v = qkv[:, :, 2].transpose(0, 2, 1, 3)
    n_blk = -(-S // MOBA_BLOCK)
    s_pad = n_blk * MOBA_BLOCK
    pad = ((0, 0), (0, 0), (0, s_pad - S), (0, 0))
    q, k, v = jnp.pad(q, pad), jnp.pad(k, pad), jnp.pad(v, pad)
    k_blocks = k.reshape(B, H, n_blk, MOBA_BLOCK, Dh)
    v_blocks = v.reshape(B, H, n_blk, MOBA_BLOCK, Dh)
    k_mean = k_blocks.astype(jnp.float32).mean(axis=3).astype(k.dtype)
    gate = jnp.einsum('bhsd,bhnd->bhsn', q, k_mean).astype(jnp.float32)
    q_blk = jnp.arange(s_pad) // MOBA_BLOCK
    past = jnp.arange(n_blk)[None, :] < q_blk[:, None]
    gate = jnp.where(past, gate, -jnp.inf)
    n_sel = min(MOBA_TOPK, n_blk)
    _, sel_idx = lax.top_k(gate, n_sel)
    sel_valid = sel_idx < q_blk[None, None, :, None]
    scale = Dh ** -0.5
    b_ix = jnp.arange(B)[:, None, None, None]
    h_ix = jnp.arange(H)[None, :, None, None]

    def one_chunk(c):
        start = c * MOBA_QCHUNK
        qs = lax.dynamic_slice_in_dim(q, start, MOBA_QCHUNK, axis=2)
        idx = lax.dynamic_slice_in_dim(sel_idx, start, MOBA_QCHUNK, axis=2)
        valid = lax.dynamic_slice_in_dim(sel_valid, start, MOBA_QCHUNK, axis=2)
        blk_start = (start // MOBA_BLOCK) * MOBA_BLOCK
        k_own = lax.dynamic_slice_in_dim(k, blk_start, MOBA_BLOCK, axis=2)
        v_own = lax.dynamic_slice_in_dim(v, blk_start, MOBA_BLOCK, axis=2)
        q_pos = start + jnp.arange(MOBA_QCHUNK)
        k_pos = blk_start + jnp.arange(MOBA_BLOCK)
        s_own = jnp.einsum('bhqd,bhkd->bhqk', qs, k_own).astype(jnp.float32) * scale
        s_own = jnp.where(k_pos[None, :] <= q_pos[:, None], s_own, -jnp.inf)
        k_sel = k_blocks[b_ix, h_ix, idx]
        v_sel = v_blocks[b_ix, h_ix, idx]
        s_sel = jnp.einsum('bhqd,bhqjkd->bhqjk', qs, k_sel).astype(jnp.float32) * scale
        s_sel = jnp.where(valid[..., None], s_sel, -jnp.inf)
        s_all = jnp.concatenate(
            [s_own, s_sel.reshape(B, H, MOBA_QCHUNK, n_sel * MOBA_BLOCK)], axis=-1)
        p = jax.nn.softmax(s_all, axis=-1).astype(v.dtype)
        p_own = p[..., :MOBA_BLOCK]
        p_sel = p[..., MOBA_BLOCK:].reshape(B, H, MOBA_QCHUNK, n_sel, MOBA_BLOCK)
        return (jnp.einsum('bhqk,bhkd->bhqd', p_own, v_own)
                + jnp.einsum('bhqjk,bhqjkd->bhqd', p_sel, v_sel))

    o = lax.map(one_chunk, jnp.arange(s_pad // MOBA_QCHUNK))
    o = o.transpose(1, 0, 3, 2, 4).reshape(B, s_pad, H * Dh)[:, :S]
    return o @ w_out


def diff_attn_mixer(x, w_in, w_out, lam_q1, lam_k1, lam_q2, lam_k2, subln_g, cos, sin, layer_idx):
    B, S, _ = x.shape
    H, dh = B_HEADS, B_HEAD_DIM
    qkv = x @ w_in
    q = apply_rotary(qkv[..., :D_MODEL].reshape(B, S, 2 * H, dh), cos, sin)
    k = apply_rotary(qkv[..., D_MODEL:2 * D_MODEL].reshape(B, S, 2 * H, dh), cos, sin)
    v = qkv[..., 2 * D_MODEL:].reshape(B, S, H, 2 * dh).transpose(0, 2, 1, 3)
    q = q.transpose(0, 2, 1, 3).reshape(B, H, 2, S, dh)
    k = k.transpose(0, 2, 1, 3).reshape(B, H, 2, S, dh)
    lam_init = 0.8 - 0.6 * math.exp(-0.3 * layer_idx)
    lam = (jnp.exp(jnp.sum(lam_q1.astype(jnp.float32) * lam_k1.astype(jnp.float32)))
           - jnp.exp(jnp.sum(lam_q2.astype(jnp.float32) * lam_k2.astype(jnp.float32)))
           + lam_init)
    scale = dh ** -0.5
    k_pos = jnp.arange(S)

    def one_block(c):
        start = c * B_QBLOCK
        qs = lax.dynamic_slice_in_dim(q, start, B_QBLOCK, axis=3)
        s = jnp.einsum('bhiqd,bhikd->bhiqk', qs, k).astype(jnp.float32) * scale
        q_pos = start + jnp.arange(B_QBLOCK)
        s = jnp.where(k_pos[None, :] <= q_pos[:, None], s, -jnp.inf)
        p = jax.nn.softmax(s, axis=-1)
        a = p[:, :, 0] - lam * p[:, :, 1]
        return jnp.einsum('bhqk,bhkd->bhqd', a.astype(v.dtype), v)

    o = lax.map(one_block, jnp.arange(S // B_QBLOCK))
    o = o.transpose(1, 0, 3, 2, 4).reshape(B, S, H, 2 * dh)
    o = rmsnorm(o, subln_g, eps=SUBLN_EPS) * (1.0 - lam_init)
    return o.reshape(B, S, H * 2 * dh) @ w_out


def rglru_mixer(x, w_in, conv_w, conv_b, w_a, b_a, w_x, b_x, lam, w_out):
    B, S, _ = x.shape
    xy = x @ w_in
    gate_branch = jax.nn.gelu(xy[..., :C_WIDTH])
    rec = causal_dwconv(xy[..., C_WIDTH:], conv_w, conv_b)
    xb = rec.reshape(B, S, C_BLOCKS, C_BLOCK_DIM)
    r = jax.nn.sigmoid(jnp.einsum('bsgi,gij->bsgj', xb, w_a) + b_a).reshape(B, S, C_WIDTH)
    i = jax.nn.sigmoid(jnp.einsum('bsgi,gij->bsgj', xb, w_x) + b_x).reshape(B, S, C_WIDTH)
    log_a = -LRU_C * r.astype(jnp.float32) * jax.nn.softplus(-lam.astype(jnp.float32))
    a = jnp.exp(log_a)
    mult = jnp.sqrt(-jnp.expm1(2.0 * log_a))
    b = mult * (i * rec).astype(jnp.float32)

    def combine(left, right):
        a1, b1 = left
        a2, b2 = right
        return a1 * a2, a2 * b1 + b2

    _, h = lax.associative_scan(combine, (a, b), axis=1)
    y = h.astype(x.dtype) * gate_branch
    return y @ w_out


def retention_mixer(x, w_in, gn_g, w_out, cos, sin):
    B, S, _ = x.shape
    H, dk, dv = D_HEADS, D_QK_DIM, D_V_DIM
    proj = x @ w_in
    q = apply_rotary(proj[..., :H * dk].reshape(B, S, H, dk), cos, sin)
    k = apply_rotary(proj[..., H * dk:2 * H * dk].reshape(B, S, H, dk), cos, sin) * (dk ** -0.5)
    v = proj[..., 2 * H * dk:2 * H * dk + H * dv].reshape(B, S, H, dv)
    g = proj[..., 2 * H * dk + H * dv:]
    n_chunk = S // RET_CHUNK

    def to_chunks(t):
        return t.reshape(B, n_chunk, RET_CHUNK, H, t.shape[-1]).transpose(1, 0, 3, 2, 4).astype(jnp.float32)

    log_gamma = jnp.log1p(-jnp.exp2(-5.0 - jnp.arange(H, dtype=jnp.float32)))
    pos = jnp.arange(RET_CHUNK, dtype=jnp.float32)
    diff = pos[:, None] - pos[None, :]
    decay_in = jnp.where(diff >= 0, jnp.exp(jnp.maximum(diff, 0.0)[None] * log_gamma[:, None, None]), 0.0)
    xi = jnp.exp((pos + 1.0)[None, :] * log_gamma[:, None])
    zeta = jnp.exp((RET_CHUNK - 1.0 - pos)[None, :] * log_gamma[:, None])
    gamma_chunk = jnp.exp(RET_CHUNK * log_gamma)

    def step(state, qkv_c):
        qc, kc, vc = qkv_c
        inner = jnp.einsum('bhqd,bhkd->bhqk', qc, kc) * decay_in
        o = (jnp.einsum('bhqk,bhkv->bhqv', inner, vc)
             + jnp.einsum('bhqd,bhdv->bhqv', qc, state) * xi[None, :, :, None])
        state = (gamma_chunk[None, :, None, None] * state
                 + jnp.einsum('bhkd,bhkv->bhdv', kc * zeta[None, :, :, None], vc))
        return state, o

    state0 = jnp.zeros((B, H, dk, dv), jnp.float32)
    _, o = lax.scan(step, state0, (to_chunks(q), to_chunks(k), to_chunks(v)))
    o = o.transpose(1, 0, 3, 2, 4).reshape(B, S, H, dv)
    mu = jnp.mean(o, axis=-1, keepdims=True)
    var = jnp.mean(jnp.square(o - mu), axis=-1, keepdims=True)
    o = (o - mu) * lax.rsqrt(var + GN_EPS) * gn_g.astype(jnp.float32).reshape(H, dv)
    y = jax.nn.silu(g) * o.reshape(B, S, H * dv).astype(x.dtype)
    return y @ w_out


def conv_ffn(x, w_in, conv_w, conv_b, w_out):
    h = x @ w_in
    gate = causal_dwconv(h[..., :D_FF], conv_w, conv_b)
    return (jax.nn.silu(gate) * h[..., D_FF:]) @ w_out


def setup_inputs(seed: int = 0) -> dict:
    key = jax.random.key(seed)
    ks = iter(jax.random.split(key, 40))
    f32 = jnp.float32

    def normal(shape, scale):
        return jax.random.normal(next(ks), shape, f32) * scale

    def gain(shape):
        return 1.0 + normal(shape, 0.02)

    n_a, n_b, n_c, n_d = (len(range(m, DEPTH, N_MIXERS)) for m in range(N_MIXERS))
    D = D_MODEL
    x = normal((BATCH, SEQ, D), 1.0)
    norm_mix_g = gain((DEPTH, D))
    norm_ffn_g = gain((DEPTH, D))
    norm_final_g = gain((D,))
    a_w_in = normal((n_a, D, 3 * A_HEADS * A_HEAD_DIM), D ** -0.5)
    a_w_out = normal((n_a, A_HEADS * A_HEAD_DIM, D), D ** -0.5)
    b_w_in = normal((n_b, D, 3 * D), D ** -0.5)
    b_w_out = normal((n_b, D, D), D ** -0.5)
    b_lam_q1 = normal((n_b, B_HEAD_DIM), 0.1)
    b_lam_k1 = normal((n_b, B_HEAD_DIM), 0.1)
    b_lam_q2 = normal((n_b, B_HEAD_DIM), 0.1)
    b_lam_k2 = normal((n_b, B_HEAD_DIM), 0.1)
    b_subln_g = gain((n_b, 2 * B_HEAD_DIM))
    c_w_in = normal((n_c, D, 2 * C_WIDTH), D ** -0.5)
    c_conv_w = normal((n_c, C_CONV, C_WIDTH), C_CONV ** -0.5)
    c_conv_b = normal((n_c, C_WIDTH), 0.01)
    c_w_a = normal((n_c, C_BLOCKS, C_BLOCK_DIM, C_BLOCK_DIM), C_BLOCK_DIM ** -0.5)
    c_b_a = normal((n_c, C_BLOCKS, C_BLOCK_DIM), 0.01)
    c_w_x = normal((n_c, C_BLOCKS, C_BLOCK_DIM, C_BLOCK_DIM), C_BLOCK_DIM ** -0.5)
    c_b_x = normal((n_c, C_BLOCKS, C_BLOCK_DIM), 0.01)
    u = jax.random.uniform(next(ks), (n_c, C_WIDTH), f32, 0.9, 0.999)
    a0 = u ** (1.0 / LRU_C)
    c_lambda = jnp.log(a0) - jnp.log1p(-a0)
    c_w_out = normal((n_c, C_WIDTH, D), C_WIDTH ** -0.5)
    d_w_in = normal((n_d, D, 2 * D_HEADS * D_QK_DIM + 2 * D_HEADS * D_V_DIM), D ** -0.5)
    d_gn_g = gain((n_d, D_HEADS * D_V_DIM))
    d_w_out = normal((n_d, D_HEADS * D_V_DIM, D), (D_HEADS * D_V_DIM) ** -0.5)
    ffn_w_in = normal((DEPTH, D, 2 * D_FF), D ** -0.5)
    ffn_conv_w = normal((DEPTH, FFN_CONV, D_FF), FFN_CONV ** -0.5)
    ffn_conv_b = normal((DEPTH, D_FF), 0.01)
    ffn_w_out = normal((DEPTH, D_FF, D), D_FF ** -0.5)
    return {
        "x": x, "norm_mix_g": norm_mix_g, "norm_ffn_g": norm_ffn_g, "norm_final_g": norm_final_g,
        "a_w_in": a_w_in, "a_w_out": a_w_out,
        "b_w_in": b_w_in, "b_w_out": b_w_out, "b_lam_q1": b_lam_q1, "b_lam_k1": b_lam_k1,
        "b_lam_q2": b_lam_q2, "b_lam_k2": b_lam_k2, "b_subln_g": b_subln_g,
        "c_w_in": c_w_in, "c_conv_w": c_conv_w, "c_conv_b": c_conv_b, "c_w_a": c_w_a, "c_b_a": c_b_a,
        "c_w_x": c_w_x, "c_b_x": c_b_x, "c_lambda": c_lambda, "c_w_out": c_w_out,
        "d_w_in": d_w_in, "d_gn_g": d_gn_g, "d_w_out": d_w_out,
        "ffn_w_in": ffn_w_in, "ffn_conv_w": ffn_conv_w, "ffn_conv_b": ffn_conv_b, "ffn_w_out": ffn_w_out,
    }


def reference(x, norm_mix_g, norm_ffn_g, norm_final_g,
              a_w_in, a_w_out,
              b_w_in, b_w_out, b_lam_q1, b_lam_k1, b_lam_q2, b_lam_k2, b_subln_g,
              c_w_in, c_conv_w, c_conv_b, c_w_a, c_b_a, c_w_x, c_b_x, c_lambda, c_w_out,
              d_w_in, d_gn_g, d_w_out,
              ffn_w_in, ffn_conv_w, ffn_conv_b, ffn_w_out):
    S = x.shape[1]
    cos_a, sin_a = rope_tables(S, A_HEAD_DIM // ROPE_FRACTION, ROPE_THETA)
    cos_b, sin_b = rope_tables(S, B_HEAD_DIM // ROPE_FRACTION, ROPE_THETA)
    cos_d, sin_d = rope_tables(S, D_QK_DIM, RET_THETA)
    h = x
    for i in range(DEPTH):
        m, j = i % N_MIXERS, i // N_MIXERS
        hn = rmsnorm(h, norm_mix_g[i])
        if m == 0:
            mix = moba_mixer(hn, a_w_in[j], a_w_out[j], cos_a, sin_a)
        elif m == 1:
            mix = diff_attn_mixer(hn, b_w_in[j], b_w_out[j], b_lam_q1[j], b_lam_k1[j],
                                  b_lam_q2[j], b_lam_k2[j], b_subln_g[j], cos_b, sin_b, i)
        elif m == 2:
            mix = rglru_mixer(hn, c_w_in[j], c_conv_w[j], c_conv_b[j], c_w_a[j], c_b_a[j],
                              c_w_x[j], c_b_x[j], c_lambda[j], c_w_out[j])
        else:
            mix = retention_mixer(hn, d_w_in[j], d_gn_g[j], d_w_out[j], cos_d, sin_d)
        h = h + mix
        h = h + conv_ffn(rmsnorm(h, norm_ffn_g[i]), ffn_w_in[i], ffn_conv_w[i], ffn_conv_b[i], ffn_w_out[i])
    return rmsnorm(h, norm_final_g)
```

```python
import math
from contextlib import ExitStack

import numpy as np
import ml_dtypes
import concourse.bass as bass
import concourse.mybir as mybir
from concourse.bass_utils import run_bass_kernel_spmd

F32 = mybir.dt.float32
BF16 = mybir.dt.bfloat16
AF = mybir.ActivationFunctionType
ALU = mybir.AluOpType
AX = mybir.AxisListType

S = 4096
D = 1024
NT = S // 128
D_FF = 2816
NFC = D_FF // 128
RMS_EPS = 1e-6
NEG = -30000.0
DBG_TILES = None

ENGS = ("pe", "act", "dve", "pool", "sp")
N_DMA_SEMS = 24


class T:
    def __init__(self, name, handle=None, psum=False):
        self.name = name
        self.h = handle
        self.psum = psum
        self.writer = None
        self.readers = []

    def __getitem__(self, idx):
        return self.h[idx]


class Op:
    __slots__ = ("eng", "fn", "deps", "is_dma", "signal", "idx", "sem", "val")

    def __init__(self, eng, fn, is_dma=False):
        self.eng = eng
        self.fn = fn
        self.deps = []
        self.is_dma = is_dma
        self.signal = False
        self.idx = None
        self.sem = None
        self.val = None


class Prog:
    def __init__(self, nc):
        self.nc = nc
        self.es = ExitStack()
        self.sems = {e: self.es.enter_context(nc.semaphore("sem_" + e)) for e in ENGS}
        self.dma_sems = [self.es.enter_context(nc.semaphore("dsem%d" % i)) for i in range(N_DMA_SEMS)]
        self.ops = []
        self.phase_es = None
        self.n_phase = 0

    def begin_phase(self):
        self.phase_es = ExitStack()
        self.ops = []

    def sb(self, name, shape, dtype=F32):
        h = self.phase_es.enter_context(self.nc.sbuf_tensor("%s_p%d" % (name, self.n_phase), list(shape), dtype))
        return T(name, h)

    def ps(self, name, shape, dtype=F32):
        h = self.phase_es.enter_context(self.nc.psum_tensor("%s_p%d" % (name, self.n_phase), list(shape), dtype))
        return T(name, h, psum=True)

    def _track(self, op, reads, writes):
        deps = []
        for t in reads:
            if t.writer is not None:
                deps.append(t.writer)
            if t.psum:
                deps.extend(r for r in t.readers if r.eng != op.eng)
        for t in writes:
            if t.writer is not None:
                deps.append(t.writer)
            deps.extend(t.readers)
        seen = set()
        for d in deps:
            if d is op or id(d) in seen:
                continue
            seen.add(id(d))
            if d.eng == "pe" and op.eng == "pe" and not d.is_dma and not op.is_dma:
                continue
            op.deps.append(d)
            d.signal = True
        for t in reads:
            t.readers.append(op)
        for t in writes:
            t.writer = op
            t.readers = []

    def op(self, eng, fn, reads=(), writes=()):
        o = Op(eng, fn)
        self._track(o, reads, writes)
        self.ops.append(o)
        return o

    def dma(self, out, in_, reads=(), writes=(), queue="sp"):
        o = Op(queue, lambda e, out=out, in_=in_: e.dma_start(out=out, in_=in_), is_dma=True)
        o.signal = True
        self._track(o, reads, writes)
        self.ops.append(o)
        return o

    def end_phase(self):
        nc = self.nc
        ops = self.ops
        cnt = {e: 0 for e in ENGS}
        ndma = 0
        last_on_sem = {}
        all_dma = []
        for o in ops:
            if o.is_dma:
                s = ndma % N_DMA_SEMS
                prev = last_on_sem.get(s)
                if prev is not None:
                    o.deps.append(prev)
                o.sem = self.dma_sems[s]
                o.val = 16 * (ndma // N_DMA_SEMS + 1)
                last_on_sem[s] = o
                ndma += 1
                all_dma.append(o)
            elif o.signal:
                cnt[o.eng] += 1
                o.sem = self.sems[o.eng]
                o.val = cnt[o.eng]
        per_eng = {e: [o for o in ops if o.eng == e] for e in ENGS}
        final_dma = list(last_on_sem.values())
        final_cnt = dict(cnt)
        sems = self.sems
        dma_sems = self.dma_sems

        def emit(e, eng):
            waited = {}
            for o in per_eng[e]:
                for d in o.deps:
                    key = id(d.sem)
                    if waited.get(key, 0) >= d.val:
                        continue
                    eng.wait_ge(d.sem, d.val)
                    waited[key] = d.val
                ins = o.fn(eng)
                if o.is_dma:
                    ins.then_inc(o.sem, 16)
                elif o.signal:
                    ins.then_inc(o.sem, 1)
            if e == "sp":
                for d in final_dma:
                    if waited.get(id(d.sem), 0) < d.val:
                        eng.wait_ge(d.sem, d.val)
                for e2 in ENGS:
                    if e2 != "sp" and final_cnt[e2] > 0:
                        eng.wait_ge(sems[e2], final_cnt[e2])

        with nc.Block() as block:
            @block.tensor
            def _(eng):
                emit("pe", eng)

            @block.scalar
            def _(eng):
                emit("act", eng)

            @block.vector
            def _(eng):
                emit("dve", eng)

            @block.gpsimd
            def _(eng):
                emit("pool", eng)

            @block.sync
            def _(eng):
                emit("sp", eng)

        with nc.Block() as block:
            @block.tensor
            def _(eng):
                eng.sem_clear(sems["pe"])

            @block.scalar
            def _(eng):
                eng.sem_clear(sems["act"])

            @block.vector
            def _(eng):
                eng.sem_clear(sems["dve"])

            @block.gpsimd
            def _(eng):
                eng.sem_clear(sems["pool"])

            @block.sync
            def _(eng):
                eng.sem_clear(sems["sp"])
                for s in dma_sems:
                    eng.sem_clear(s)

        self.phase_es.close()
        self.phase_es = None
        self.ops = []
        self.n_phase += 1


class Ctx:
    pass


def load_featmajor_vec(p, cx, vec_ap, n, name, ps_tile):
    st = p.sb(name + "_st", [n, 128], F32)
    out = p.sb(name, [128, n], F32)
    p.dma(st[:, :], vec_ap.rearrange("(j q) -> j q", q=128), writes=[st])
    p.op("pe", lambda e: e.transpose(ps_tile[:, 0:n], st[:, :], cx.ident_f[0:n, 0:n]),
         reads=[st, cx.ident_f_t], writes=[ps_tile])
    p.op("dve", lambda e: e.tensor_copy(out[:, :], ps_tile[:, 0:n]), reads=[ps_tile], writes=[out])
    return out


def load_consts(p, cx):
    cx.ident_f_t = p.sb("ident_f", [128, 128], F32)
    cx.ident_b_t = p.sb("ident_b", [128, 128], BF16)
    cx.ident_f = cx.ident_f_t.h
    cx.ident_b = cx.ident_b_t.h
    p.dma(cx.ident_f_t[:, :], cx.d_ident_f[:, :], writes=[cx.ident_f_t])
    p.dma(cx.ident_b_t[:, :], cx.d_ident_b[:, :], writes=[cx.ident_b_t])
    mh = p.sb("mhalf", [128, 1], F32)
    p.op("pool", lambda e: e.memset(mh[:, :], -0.5), writes=[mh])
    NORM_CONST["mhalf"] = mh


def load_weight_cols(p, W_ap, K, c0, ncols, dst, d0, stage_tiles, gT=None, col_chunk=256, engs=("dve", "act"), ctr=None):
    kcs = K // 128
    ctr = ctr if ctr is not None else [0]
    for kc in range(kcs):
        for cc in range(0, ncols, col_chunk):
            cw = min(col_chunk, ncols - cc)
            i = ctr[0]
            ctr[0] += 1
            st = stage_tiles[i % len(stage_tiles)]
            eng = engs[i % len(engs)]
            p.dma(st[:, 0:cw], W_ap[kc * 128:(kc + 1) * 128, c0 + cc:c0 + cc + cw], writes=[st])
            o = dst[:, kc, d0 + cc:d0 + cc + cw]
            if eng == "act":
                if gT is not None:
                    p.op("act", lambda e, st=st, kc=kc, cw=cw, o=o: e.activation(o, st[:, 0:cw], AF.Copy, scale=gT[:, kc:kc + 1]),
                         reads=[st, gT], writes=[dst])
                else:
                    p.op("act", lambda e, st=st, cw=cw, o=o: e.activation(o, st[:, 0:cw], AF.Copy), reads=[st], writes=[dst])
            elif gT is not None:
                p.op(eng, lambda e, st=st, kc=kc, cw=cw, o=o: e.tensor_scalar(o, st[:, 0:cw], gT[:, kc:kc + 1], 1.0, ALU.mult, ALU.mult),
                     reads=[st, gT], writes=[dst])
            else:
                p.op(eng, lambda e, st=st, cw=cw, o=o: e.tensor_copy(o, st[:, 0:cw]), reads=[st], writes=[dst])


def load_weight_bf16(p, W_ap, K, N, dst, stage_tiles, gT=None, col_chunk=2048, engs=("dve", "act")):
    kcs = K // 128
    i = 0
    for kc in range(kcs):
        for c0 in range(0, N, col_chunk):
            cw = min(col_chunk, N - c0)
            st = stage_tiles[i % len(stage_tiles)]
            eng = engs[i % len(engs)]
            i += 1
            p.dma(st[:, 0:cw], W_ap[kc * 128:(kc + 1) * 128, c0:c0 + cw], writes=[st])
            if eng == "act":
                if gT is not None:
                    p.op("act", lambda e, st=st, kc=kc, c0=c0, cw=cw: e.activation(
                        dst[:, kc, c0:c0 + cw], st[:, 0:cw], AF.Copy, scale=gT[:, kc:kc + 1]),
                        reads=[st, gT], writes=[dst])
                else:
                    p.op("act", lambda e, st=st, kc=kc, c0=c0, cw=cw: e.activation(
                        dst[:, kc, c0:c0 + cw], st[:, 0:cw], AF.Copy), reads=[st], writes=[dst])
            elif gT is not None:
                p.op(eng, lambda e, st=st, kc=kc, c0=c0, cw=cw: e.tensor_scalar(
                    dst[:, kc, c0:c0 + cw], st[:, 0:cw], gT[:, kc:kc + 1], 1.0, ALU.mult, ALU.mult),
                    reads=[st, gT], writes=[dst])
            else:
                p.op(eng, lambda e, st=st, kc=kc, c0=c0, cw=cw: e.tensor_copy(dst[:, kc, c0:c0 + cw], st[:, 0:cw]),
                     reads=[st], writes=[dst])


def norm_tile(p, hb, xn, ss, rstd, junk, eps=RMS_EPS, d=D, cx=None):
    p.op("act", lambda e: e.activation(junk[:, :], hb[:, :], AF.Square, accum_out=ss[:, 0:1]),
         reads=[hb], writes=[junk, ss])
    p.op("dve", lambda e: e.tensor_scalar(rstd[:, 0:1], ss[:, 0:1], 1.0 / d, eps, ALU.mult, ALU.add),
         reads=[ss], writes=[rstd])
    p.op("pool", lambda e: e.tensor_tensor(rstd[:, 0:1], rstd[:, 0:1], NORM_CONST["mhalf"][:, 0:1], ALU.pow),
         reads=[rstd, NORM_CONST["mhalf"]], writes=[rstd])
    p.op("act", lambda e: e.activation(xn[:, :], hb[:, :], AF.Copy, scale=rstd[:, 0:1]),
         reads=[hb, rstd], writes=[xn])


NORM_CONST = {}


def transpose_to(p, cx, src, ncol_blocks, ps_tr, dst, dst_col0, evac_eng="dve"):
    for j0 in range(0, ncol_blocks, 8):
        nb = min(8, ncol_blocks - j0)
        for j in range(nb):
            p.op("pe", lambda e, j=j, j0=j0: e.transpose(ps_tr[:, j * 128:(j + 1) * 128],
                                                       src[:, (j0 + j) * 128:(j0 + j + 1) * 128], cx.ident_b[:, :]),
                 reads=[src, cx.ident_b_t], writes=[ps_tr])
        if evac_eng == "act":
            p.op("act", lambda e, j0=j0, nb=nb: e.activation(
                dst[:, j0:j0 + nb, dst_col0:dst_col0 + 128],
                ps_tr[:, 0:nb * 128].rearrange("p (j t) -> p j t", t=128), AF.Copy),
                reads=[ps_tr], writes=[dst])
        else:
            p.op(evac_eng, lambda e, j0=j0, nb=nb: e.tensor_copy(
                dst[:, j0:j0 + nb, dst_col0:dst_col0 + 128],
                ps_tr[:, 0:nb * 128].rearrange("p (j t) -> p j t", t=128)),
                reads=[ps_tr], writes=[dst])


def ffn_phase(p, cx, layer, h_in, h_out):
    p.begin_phase()
    load_consts(p, cx)
    psb = [p.ps("psb%d" % i, [128, 512], F32) for i in range(6)]
    ps_tr = [p.ps("pstr%d" % i, [128, 1024], BF16) for i in range(2)]
    Wig = [p.sb("Wi%d" % g, [128, 8, 512], BF16) for g in range(NFC // 2)]
    Wo = p.sb("Wo", [128, NFC, D], BF16)
    stage = [p.sb("wst%d" % i, [128, 352], F32) for i in range(4)]
    gT = load_featmajor_vec(p, cx, cx.norm_ffn_g[layer, :], 8, "gT", psb[0])
    cw = [load_featmajor_vec(p, cx, cx.ffn_conv_w[layer, k, :], NFC, "cw%d" % k, psb[0]) for k in range(3)]
    cb = load_featmajor_vec(p, cx, cx.ffn_conv_b[layer, :], NFC, "cb", psb[0])
    ctr = [0]

    def load_group(g):
        load_weight_cols(p, cx.ffn_w_in[layer], D, g * 256, 256, Wig[g], 0, stage, gT=gT, col_chunk=256, ctr=ctr)
        load_weight_cols(p, cx.ffn_w_in[layer], D, D_FF + g * 256, 256, Wig[g], 256, stage, gT=gT, col_chunk=256, ctr=ctr)

    def load_wo_rows(j):
        for c0 in range(0, D, 256):
            i = ctr[0]
            ctr[0] += 1
            st = stage[i % len(stage)]
            p.dma(st[:, 0:256], cx.ffn_w_out[layer][j * 128:(j + 1) * 128, c0:c0 + 256], writes=[st])
            if i % 2 == 0:
                p.op("dve", lambda e, st=st, c0=c0: e.tensor_copy(Wo[:, j, c0:c0 + 256], st[:, 0:256]), reads=[st], writes=[Wo])
            else:
                p.op("act", lambda e, st=st, c0=c0: e.activation(Wo[:, j, c0:c0 + 256], st[:, 0:256], AF.Copy),
                     reads=[st], writes=[Wo])

    halo = p.sb("halo", [128, NFC, 2], F32)
    p.op("pool", lambda e: e.memset(halo[:, :, :], 0.0), writes=[halo])
    hn = [p.sb("hn%d" % i, [128, D], F32) for i in range(2)]
    rb = [p.sb("rb%d" % i, [128, D], F32) for i in range(2)]
    xn = [p.sb("xn%d" % i, [128, D], BF16) for i in range(2)]
    ss = [p.sb("ss%d" % i, [128, 1], F32) for i in range(2)]
    rstd = [p.sb("rstd%d" % i, [128, 1], F32) for i in range(2)]
    XnT = [p.sb("XnT%d" % i, [128, 8, 512], BF16) for i in range(2)]
    A = p.sb("AT", [128, NFC, 512], BF16)
    G = [p.sb("G%d" % i, [128, 514], F32) for i in range(2)]
    c1 = [p.sb("c1_%d" % i, [128, 512], F32) for i in range(2)]

    NTT = S // 512

    def prep_n(tt, s):
        i = tt * 4 + s
        t = hn[i % 2]
        r0 = i * 128
        p.dma(t[:, :], h_in[r0:r0 + 128, :], writes=[t])
        k = i % 2
        norm_tile(p, t, xn[k], ss[k], rstd[k], xn[k])

    def prep_t(tt, s):
        i = tt * 4 + s
        k = i % 2
        transpose_to(p, cx, xn[k], 8, ps_tr[k], XnT[tt % 2], s * 128, evac_eng="dve")

    def prep(tt):
        for s in range(4):
            prep_n(tt, s)
            prep_t(tt, s)

    prep(0)
    load_group(0)
    for tt in range(NTT):
        X = XnT[tt % 2]
        for fc in range(NFC):
            if tt == 0:
                if fc % 2 == 0 and fc // 2 + 1 < NFC // 2:
                    load_group(fc // 2 + 1)
                load_wo_rows(fc)
            if tt + 1 < NTT and fc in (4, 6, 8, 10, 12):
                sidx = (fc - 4) // 2
                if sidx < 4:
                    prep_n(tt + 1, sidx)
                if sidx >= 1:
                    prep_t(tt + 1, sidx - 1)
            b = fc % 2
            gps, ups = psb[b], psb[2 + b]
            for kc in range(8):
                p.op("pe", lambda e, kc=kc, fc=fc, gps=gps, X=X: e.matmul(
                    gps[:, :], Wig[fc // 2][:, kc, (fc % 2) * 128:(fc % 2 + 1) * 128], X[:, kc, :],
                    start=(kc == 0), stop=(kc == 7)),
                    reads=[Wig[fc // 2], X], writes=[gps])
            for kc in range(8):
                p.op("pe", lambda e, kc=kc, fc=fc, ups=ups, X=X: e.matmul(
                    ups[:, :], Wig[fc // 2][:, kc, 256 + (fc % 2) * 128:256 + (fc % 2 + 1) * 128], X[:, kc, :],
                    start=(kc == 0), stop=(kc == 7)),
                    reads=[Wig[fc // 2], X], writes=[ups])
            g = G[b]
            p.op("pool", lambda e, g=g, fc=fc: e.tensor_copy(g[:, 0:2], halo[:, fc, :]), reads=[halo], writes=[g])
            p.op("act", lambda e, g=g, gps=gps: e.activation(g[:, 2:514], gps[:, :], AF.Copy), reads=[gps], writes=[g])
            p.op("pool", lambda e, g=g, fc=fc: e.tensor_copy(halo[:, fc, :], g[:, 512:514]), reads=[g], writes=[halo])
            p.op("act", lambda e, g=g, fc=fc, b=b: e.activation(
                c1[b][:, :], g[:, 0:512], AF.Identity, bias=cb[:, fc:fc + 1], scale=cw[0][:, fc:fc + 1]),
                reads=[g, cb, cw[0]], writes=[c1[b]])
            p.op("dve", lambda e, g=g, fc=fc, b=b: e.scalar_tensor_tensor(
                c1[b][:, :], g[:, 1:513], cw[1][:, fc:fc + 1], c1[b][:, :], ALU.mult, ALU.add),
                reads=[g, cw[1], c1[b]], writes=[c1[b]])
            p.op("dve", lambda e, g=g, fc=fc, b=b: e.scalar_tensor_tensor(
                c1[b][:, :], g[:, 2:514], cw[2][:, fc:fc + 1], c1[b][:, :], ALU.mult, ALU.add),
                reads=[g, cw[2], c1[b]], writes=[c1[b]])
            p.op("act", lambda e, b=b: e.activation(c1[b][:, :], c1[b][:, :], AF.Silu), reads=[c1[b]], writes=[c1[b]])
            p.op("dve", lambda e, b=b, fc=fc, ups=ups: e.tensor_tensor(A[:, fc, :], c1[b][:, :], ups[:, :], ALU.mult),
                 reads=[c1[b], ups], writes=[A])
        for s in range(4):
            i = tt * 4 + s
            r0 = i * 128
            hbt = rb[i % 2]
            if s == 0:
                p.dma(hbt[:, :], h_in[r0:r0 + 128, :], writes=[hbt])
            if s + 1 < 4:
                p.dma(rb[(i + 1) % 2][:, :], h_in[r0 + 128:r0 + 256, :], writes=[rb[(i + 1) % 2]])
            for half in range(2):
                ops_ = psb[4 + half]
                for fc in range(NFC):
                    p.op("pe", lambda e, fc=fc, s=s, half=half, ops_=ops_: e.matmul(
                        ops_[:, :], A[:, fc, s * 128:(s + 1) * 128], Wo[:, fc, half * 512:(half + 1) * 512],
                        start=(fc == 0), stop=(fc == NFC - 1)),
                        reads=[A, Wo], writes=[ops_])
                p.op("dve", lambda e, half=half, ops_=ops_, hbt=hbt: e.tensor_tensor(
                    hbt[:, half * 512:(half + 1) * 512], hbt[:, half * 512:(half + 1) * 512], ops_[:, :], ALU.add),
                    reads=[hbt, ops_], writes=[hbt])
            p.dma(h_out[r0:r0 + 128, :], hbt[:, :], reads=[hbt])
    p.end_phase()


def rope_bank(p, ps_bank, dst, dcol0, c2d, s2d, cs_tiles, nsub, hd, half, tmp):
    src = ps_bank.h[:, :].rearrange("p (n d) -> p n d", d=hd)
    dv = dst.h[:, dcol0:dcol0 + 512].rearrange("p (n d) -> p n d", d=hd)
    x1, x2 = src[:, :, 0:half], src[:, :, half:2 * half]
    c = c2d.rearrange("p (n d) -> p n d", d=half)
    s_ = s2d.rearrange("p (n d) -> p n d", d=half)
    w = nsub * half
    tv = [tt.h[:, 0:w].rearrange("p (n d) -> p n d", d=half) for tt in tmp]
    cs = list(cs_tiles)
    p.op("dve", lambda e: e.tensor_tensor(tv[0], x1, c, ALU.mult), reads=[ps_bank] + cs, writes=[tmp[0]])
    p.op("dve", lambda e: e.tensor_tensor(tv[1], x2, s_, ALU.mult), reads=[ps_bank] + cs, writes=[tmp[1]])
    p.op("dve", lambda e: e.tensor_tensor(dv[:, :, 0:half], tv[0], tv[1], ALU.subtract),
         reads=[tmp[0], tmp[1]], writes=[dst])
    p.op("dve", lambda e: e.tensor_tensor(tv[2], x2, c, ALU.mult), reads=[ps_bank] + cs, writes=[tmp[2]])
    p.op("dve", lambda e: e.tensor_tensor(tv[3], x1, s_, ALU.mult), reads=[ps_bank] + cs, writes=[tmp[3]])
    p.op("dve", lambda e: e.tensor_tensor(dv[:, :, half:2 * half], tv[2], tv[3], ALU.add),
         reads=[tmp[2], tmp[3]], writes=[dst])


def proj_phase(p, cx, layer, W_ap, N, h_in, setup, epi, tile_done):
    p.begin_phase()
    load_consts(p, cx)
    banks = [p.ps("pb%d" % i, [128, 512], F32) for i in range(6)]
    ps_tr = [p.ps("pstr%d" % i, [128, 1024], BF16) for i in range(2)]
    Wp = [p.sb("W%d" % i, [128, 8, 1536], BF16) for i in range(N // 1536)]
    stage = [p.sb("wst%d" % i, [128, 768], F32) for i in range(3)]
    gT = load_featmajor_vec(p, cx, cx.norm_mix_g[layer, :], 8, "gT", banks[0])
    ctr = [0]

    def load_pass(i):
        load_weight_cols(p, W_ap, D, i * 1536, 1536, Wp[i], 0, stage, gT=gT, col_chunk=768, ctr=ctr)
    st = setup(p, ps_tr)
    hn = [p.sb("hn%d" % i, [128, D], F32) for i in range(2)]
    xn = [p.sb("xn%d" % i, [128, D], BF16) for i in range(2)]
    ss = [p.sb("ss%d" % i, [128, 1], F32) for i in range(2)]
    rstd = [p.sb("rstd%d" % i, [128, 1], F32) for i in range(2)]
    XT = [p.sb("XT%d" % i, [128, 8, 128], BF16) for i in range(2)]
    npass = N // 1536
    ntl = NT if DBG_TILES is None else DBG_TILES

    def prep(t):
        k = t % 2
        p.dma(hn[k][:, :], h_in[t * 128:(t + 1) * 128, :], writes=[hn[k]])
        norm_tile(p, hn[k], xn[k], ss[k], rstd[k], xn[k])
        transpose_to(p, cx, xn[k], 8, ps_tr[0], XT[k], 0, evac_eng="act")

    prep(0)
    load_pass(0)
    for t in range(ntl):
        k = t % 2
        for ps_ in range(npass):
            if t == 0 and ps_ + 1 < npass:
                load_pass(ps_ + 1)
            bs = banks[(ps_ % 2) * 3:(ps_ % 2) * 3 + 3]
            for kc in range(8):
                for j in range(3):
                    c0 = j * 512
                    Wt = Wp[ps_]
                    p.op("pe", lambda e, kc=kc, j=j, c0=c0, bs=bs, k=k, Wt=Wt: e.matmul(
                        bs[j][:, :], XT[k][:, kc, :], Wt[:, kc, c0:c0 + 512], start=(kc == 0), stop=(kc == 7)),
                        reads=[XT[k], Wt], writes=[bs[j]])
            if ps_ == 0 and t + 1 < ntl:
                prep(t + 1)
            for j in range(3):
                epi(p, st, t, ps_ * 3 + j, bs[j])
        tile_done(p, st, t, ps_tr[1])
    p.end_phase()


def load_rope(p, cx, cos_ap, sin_ap, half):
    cosT = p.sb("cosT", [128, NT, half], F32)
    sinT = p.sb("sinT", [128, NT, half], F32)
    p.dma(cosT[:, :, :], cos_ap, writes=[cosT])
    p.dma(sinT[:, :, :], sin_ap, writes=[sinT])
    return cosT, sinT


def qkv_proj_phase(p, cx, layer, W_ap, h_in, cos_ap, sin_ap, nsub, hd, half, QT_d, KT_d, V_d):
    def setup(p, ps_tr):
        st = Ctx()
        st.cosT, st.sinT = load_rope(p, cx, cos_ap, sin_ap, nsub * half)
        st.qk = [p.sb("qk%d" % i, [128, 2048], BF16) for i in range(2)]
        st.vb = [p.sb("vb%d" % i, [128, 1024], BF16) for i in range(2)]
        st.tmp = [p.sb("rt%d" % i, [128, 64], F32) for i in range(4)]
        st.QKT = [p.sb("QKT%d" % i, [128, 16, 512], BF16) for i in range(2)]
        return st

    def epi(p, st, t, gb, bank):
        k = t % 2
        if gb < 4:
            p.op("act", lambda e: e.activation(st.qk[k][:, gb * 512:(gb + 1) * 512], bank[:, :], AF.Copy),
                 reads=[bank], writes=[st.qk[k]])
            rope_bank(p, bank, st.qk[k], gb * 512, st.cosT[:, t, :], st.sinT[:, t, :], [st.cosT, st.sinT], nsub, hd, half, st.tmp)
        else:
            p.op("act", lambda e: e.activation(st.vb[k][:, (gb - 4) * 512:(gb - 3) * 512], bank[:, :], AF.Copy),
                 reads=[bank], writes=[st.vb[k]])

    def tile_done(p, st, t, ps_tr):
        k = t % 2
        p.dma(V_d[t * 128:(t + 1) * 128, :], st.vb[k][:, :], reads=[st.vb[k]])
        g = (t // 4) % 2
        transpose_to(p, cx, st.qk[k], 16, ps_tr, st.QKT[g], (t % 4) * 128, evac_eng="act")
        if t % 4 == 3:
            t0 = (t // 4) * 512
            for hh in range(8):
                p.dma(QT_d[hh, :, t0:t0 + 512], st.QKT[g][:, hh, :], reads=[st.QKT[g]])
                p.dma(KT_d[hh, :, t0:t0 + 512], st.QKT[g][:, 8 + hh, :], reads=[st.QKT[g]])

    proj_phase(p, cx, layer, W_ap, 3072, h_in, setup, epi, tile_done)


def out_proj_residual(p, cx, OT, nk, Wo, h_in, h_out, tok0, rb, ops2, cnt):
    nb = len(rb)
    base = cnt[0]
    cnt[0] += 4

    def load(s):
        r0 = tok0 + s * 128
        hbt = rb[(base + s) % nb]
        p.dma(hbt[:, :], h_in[r0:r0 + 128, :], writes=[hbt])

    load(0)
    for s in range(4):
        if s + 1 < 4:
            load(s + 1)
        r0 = tok0 + s * 128
        hbt = rb[(base + s) % nb]
        for half in range(2):
            ops_ = ops2[half]
            for kc in range(nk):
                p.op("pe", lambda e, kc=kc, s=s, half=half, ops_=ops_: e.matmul(
                    ops_[:, :], OT[:, kc, s * 128:(s + 1) * 128], Wo[:, kc, half * 512:(half + 1) * 512],
                    start=(kc == 0), stop=(kc == nk - 1)),
                    reads=[OT, Wo], writes=[ops_])
            p.op("dve", lambda e, half=half, ops_=ops_, hbt=hbt: e.tensor_tensor(
                hbt[:, half * 512:(half + 1) * 512], hbt[:, half * 512:(half + 1) * 512], ops_[:, :], ALU.add),
                reads=[hbt, ops_], writes=[hbt])
        p.dma(h_out[r0:r0 + 128, :], hbt[:, :], reads=[hbt])


def run_skewed(items, la, hooks=None):
    n = len(items)
    for i in range(min(la, n)):
        items[i][0]()
    for j in range(n):
        items[j][1]()
        if j + la < n:
            items[j + la][0]()
        if hooks and j in hooks:
            hk = hooks[j]
            for f in (hk if isinstance(hk, list) else [hk]):
                f()


def moba_attn_phase(p, cx, h_in, h_out, QT_d, KT_d, V_d):
    p.begin_phase()
    load_consts(p, cx)
    sT = [p.ps("sT%d" % i, [128, 512], F32) for i in range(2)]
    oT = [p.ps("oT%d" % i, [128, 512], F32) for i in range(2)]
    sm = [p.ps("sm%d" % i, [128, 512], F32) for i in range(2)]
    misc = p.ps("misc", [128, 512], F32)
    ops1 = p.ps("ops1", [128, 512], F32)
    KTh = [p.sb("KT%d" % h, [128, S], BF16) for h in range(8)]
    Vg = [p.sb("V%d" % g, [128, 4, D], BF16) for g in range(8)]
    Wo = p.sb("Wo", [128, 8, D], BF16)
    stage = [p.sb("wst%d" % i, [128, 256], F32) for i in range(2)]
    for h in range(8):
        p.dma(KTh[h][:, :], KT_d[h, :, :], writes=[KTh[h]])
    for g in range(8):
        for t4 in range(4 * g, 4 * g + 4):
            p.dma(Vg[g][:, t4 % 4, :], V_d[t4 * 128:(t4 + 1) * 128, :], writes=[Vg[g]])
    load_weight_bf16(p, cx.a_w_out, D, D, Wo, stage, col_chunk=256)
    ones_b = p.sb("ones_b", [128, 128], BF16)
    p.op("pool", lambda e: e.memset(ones_b[:, :], 1.0), writes=[ones_b])
    CM = p.sb("CM", [128, 4, 512], BF16)
    p.dma(CM[:, :, :], cx.d_cmask[:, :, :], writes=[CM])
    En = p.sb("En", [128, 16, 128], BF16)
    p.op("pool", lambda e: e.memset(En[:, :, :], 0.0), writes=[En])
    p.dma(En[0:16, :, :], cx.d_en[:, :, :], writes=[En])
    PB2 = p.sb("PB2", [128, 16, 16], F32)
    p.dma(PB2[:, :, :], cx.d_pb2[:, :, :], writes=[PB2])
    kmf = p.sb("kmf", [128, 8, 16], F32)
    kmT = p.sb("kmT", [128, 8, 16], BF16)
    for h in range(8):
        p.op("dve", lambda e, h=h: e.tensor_reduce(
            kmf[:, h, :], KTh[h][:, :].rearrange("p (n k) -> p n k", k=256), AX.X, ALU.add),
            reads=[KTh[h]], writes=[kmf])
    p.op("dve", lambda e: e.tensor_scalar(kmT[:, :, :], kmf[:, :, :], 1.0 / 256, None, ALU.mult), reads=[kmf], writes=[kmT])

    QT = [p.sb("QT%d" % i, [128, 8, 512], BF16) for i in range(2)]
    gm = [p.sb("gm%d" % i, [128, 16], F32) for i in range(2)]
    top8 = p.sb("top8", [128, 8, 8], F32)
    thr = p.sb("thr", [128, 8], F32)
    bft = p.sb("bft", [128, 8, 16], F32)
    btok = [p.sb("btok%d" % i, [128, 8, 16], BF16) for i in range(4)]
    bT = p.sb("biasT", [128, 8, 512], BF16)
    p.op("pool", lambda e: e.memset(bT[:, :, :], 0.0), writes=[bT])
    PT = [p.sb("PT%d" % i, [128, 512], BF16) for i in range(3)]
    rs = p.sb("rs", [128, 512], F32)
    OT = p.sb("OT", [128, 8, 512], BF16)
    rb = [p.sb("rb%d" % i, [128, D], F32) for i in range(2)]
    cnt = [0]
    scale = 128.0 ** -0.5
    NQ = S // 512 if DBG_TILES is None else DBG_TILES // 4
    pic = [0]
    misc_b = misc.h[:, :].bitcast(BF16)

    def load_q(Q):
        for hh in range(8):
            p.dma(QT[Q % 2][:, hh, :], QT_d[hh, :, Q * 512:(Q + 1) * 512], writes=[QT[Q % 2]])

    def gate1(Q):
        q = QT[Q % 2]
        for s in range(4):
            for h in range(8):
                p.op("pe", lambda e, s=s, h=h: e.matmul(
                    misc[:, (s * 8 + h) * 16:(s * 8 + h + 1) * 16], q[:, h, s * 128:(s + 1) * 128], kmT[:, h, :],
                    start=True, stop=True), reads=[q, kmT], writes=[misc])
        for s in range(4):
            qb = 2 * Q + s // 2
            bt = btok[s]
            for h in range(8):
                g_ = gm[h % 2]
                p.op("dve", lambda e, s=s, h=h, g_=g_, qb=qb: e.tensor_tensor(
                    g_[:, :], misc[:, (s * 8 + h) * 16:(s * 8 + h + 1) * 16], PB2[:, qb, :], ALU.add),
                    reads=[misc, PB2], writes=[g_])
                p.op("dve", lambda e, h=h, g_=g_: e.max(top8[:, h, :], g_[:, :]), reads=[g_], writes=[top8])
                p.op("dve", lambda e, h=h: e.tensor_scalar(thr[:, h:h + 1], top8[:, h, 3:4], -1e29, None, ALU.max),
                     reads=[top8], writes=[thr])
                p.op("dve", lambda e, h=h, g_=g_: e.tensor_scalar(
                    bft[:, h, :], g_[:, :], thr[:, h:h + 1], -NEG, ALU.is_ge, ALU.mult),
                    reads=[g_, thr], writes=[bft])
            p.op("dve", lambda e, bt=bt: e.tensor_scalar(bt[:, :, :], bft[:, :, :], NEG, None, ALU.add),
                 reads=[bft], writes=[bt])

    def gate2(Q):
        for s in range(4):
            bt = btok[s]
            for h in range(8):
                p.op("pe", lambda e, h=h, bt=bt: e.transpose(
                    misc_b[0:16, h * 128:(h + 1) * 128], bt[:, h, :], cx.ident_b[:, :]),
                    reads=[bt, cx.ident_b_t], writes=[misc])
            p.op("act", lambda e, s=s: e.activation(
                bT[0:16, :, s * 128:(s + 1) * 128],
                misc_b[0:16, 0:1024].rearrange("p (h t) -> p h t", t=128), AF.Copy),
                reads=[misc], writes=[bT])

    def attention(Q, mid_hook):
        q = QT[Q % 2]
        items = []
        nkt = 4 * Q + 4
        for h in range(8):
            o_ps, s_ps = oT[h % 2], sm[h % 2]
            for kt in range(nkt):
                def sa(h=h, kt=kt):
                    pi = pic[0]
                    pic[0] += 1
                    st_, pt = sT[pi % 2], PT[pi % 3]
                    diag = kt >= 4 * Q
                    p.op("pe", lambda e: e.matmul(
                        st_[:, :], KTh[h][:, kt * 128:(kt + 1) * 128], q[:, h, :], start=True, stop=False),
                        reads=[KTh[h], q], writes=[st_])
                    p.op("pe", lambda e: e.matmul(
                        st_[:, :], En[:, kt // 2, :], bT[:, h, :], start=False, stop=(not diag)),
                        reads=[En, bT], writes=[st_])
                    if diag:
                        p.op("pe", lambda e: e.matmul(
                            st_[:, :], cx.ident_b[:, :], CM[:, kt - 4 * Q, :], start=False, stop=True),
                            reads=[CM, cx.ident_b_t], writes=[st_])
                    p.op("act", lambda e: e.activation(pt[:, :], st_[:, :], AF.Exp, scale=scale),
                         reads=[st_], writes=[pt])
                    return pt

                def sb_(h=h, kt=kt, o_ps=o_ps, s_ps=s_ps, holder=None):
                    pt = holder[0]
                    p.op("pe", lambda e: e.matmul(
                        o_ps[:, :], Vg[kt // 4][:, kt % 4, h * 128:(h + 1) * 128], pt[:, :],
                        start=(kt == 0), stop=(kt == nkt - 1)),
                        reads=[Vg[kt // 4], pt], writes=[o_ps])
                    p.op("pe", lambda e: e.matmul(
                        s_ps[:, :], ones_b[:, :], pt[:, :], start=(kt == 0), stop=(kt == nkt - 1)),
                        reads=[ones_b, pt], writes=[s_ps])
                    if kt == nkt - 1:
                        p.op("dve", lambda e: e.reciprocal(rs[:, :], s_ps[:, :]), reads=[s_ps], writes=[rs])
                        p.op("dve", lambda e: e.tensor_tensor(OT[:, h, :], o_ps[:, :], rs[:, :], ALU.mult),
                             reads=[o_ps, rs], writes=[OT])

                holder = [None]
                items.append(((lambda sa=sa, holder=holder: holder.__setitem__(0, sa())),
                              (lambda sb_=sb_, holder=holder: sb_(holder=holder))))
        hooks = {len(items) // 2: mid_hook} if mid_hook else None
        run_skewed(items, 2, hooks)

    load_q(0)
    gate1(0)
    gate2(0)
    for Q in range(NQ):
        nxt = None
        if Q + 1 < NQ:
            load_q(Q + 1)
            nxt = (lambda Q=Q: gate1(Q + 1))
        attention(Q, nxt)
        out_proj_residual(p, cx, OT, 8, Wo, h_in, h_out, Q * 512, rb, [ops1, ops1], cnt)
        if Q + 1 < NQ:
            gate2(Q + 1)
    p.end_phase()


def diff_attn_phase(p, cx, h_in, h_out, QT_d, KT_d, V_d):
    p.begin_phase()
    load_consts(p, cx)
    sT = [p.ps("sT%d" % i, [128, 512], F32) for i in range(2)]
    oT = [p.ps("oT%d" % i, [128, 512], F32) for i in range(2)]
    sm = [p.ps("sm%d" % i, [128, 512], F32) for i in range(2)]
    ops2 = [p.ps("ops%d" % i, [128, 512], F32) for i in range(2)]
    KTh = [p.sb("KT%d" % h, [128, S], BF16) for h in range(8)]
    Vg = [p.sb("V%d" % g, [128, 4, D], BF16) for g in range(8)]
    Wo = p.sb("Wo", [128, 8, D], BF16)
    stage = [p.sb("wst%d" % i, [128, 512], F32) for i in range(2)]
    for g in range(8):
        p.dma(KTh[g][:, :], KT_d[g, :, :], writes=[KTh[g]])
        for t4 in range(4 * g, 4 * g + 4):
            p.dma(Vg[g][:, t4 % 4, :], V_d[t4 * 128:(t4 + 1) * 128, :], writes=[Vg[g]])
    load_weight_bf16(p, cx.b_w_out, D, D, Wo, stage, col_chunk=512)
    ones_b = p.sb("ones_b", [128, 128], BF16)
    p.op("pool", lambda e: e.memset(ones_b[:, :], 1.0), writes=[ones_b])
    CM = p.sb("CM", [128, 4, 512], BF16)
    p.dma(CM[:, :, :], cx.d_cmask[:, :, :], writes=[CM])
    lam_init = 0.8 - 0.6 * math.exp(-0.3 * 1)
    lv = p.sb("lv", [1, 4, 64], F32)
    for i, nm in enumerate(("b_lam_q1", "b_lam_k1", "b_lam_q2", "b_lam_k2")):
        p.dma(lv[:, i, :], getattr(cx, nm).rearrange("(o d) -> o d", o=1), writes=[lv])
    pr = p.sb("pr", [1, 2, 64], F32)
    p.op("dve", lambda e: e.tensor_tensor(pr[:, 0, :], lv[:, 0, :], lv[:, 1, :], ALU.mult), reads=[lv], writes=[pr])
    p.op("dve", lambda e: e.tensor_tensor(pr[:, 1, :], lv[:, 2, :], lv[:, 3, :], ALU.mult), reads=[lv], writes=[pr])
    sv = p.sb("sv", [1, 2], F32)
    p.op("dve", lambda e: e.tensor_reduce(sv[:, :], pr[:, :, :], AX.X, ALU.add), reads=[pr], writes=[sv])
    p.op("act", lambda e: e.activation(sv[:, :], sv[:, :], AF.Exp), reads=[sv], writes=[sv])
    nl = p.sb("nl", [1, 2], F32)
    p.op("dve", lambda e: e.tensor_tensor(nl[:, 0:1], sv[:, 1:2], sv[:, 0:1], ALU.subtract), reads=[sv], writes=[nl])
    p.op("dve", lambda e: e.tensor_scalar(nl[:, 0:1], nl[:, 0:1], -lam_init, None, ALU.add), reads=[nl], writes=[nl])
    p.op("dve", lambda e: e.tensor_copy(nl[:, 1:2], nl[:, 0:1]), reads=[nl], writes=[nl])
    ones_f = p.sb("ones_f", [1, 128], F32)
    p.op("pool", lambda e: e.memset(ones_f[:, :], 1.0), writes=[ones_f])
    p.op("pe", lambda e: e.matmul(ops2[0][:, 0:2], ones_f[:, :], nl[:, :], start=True, stop=True),
         reads=[ones_f, nl], writes=[ops2[0]])
    nlam = p.sb("nlam", [128, 1], F32)
    p.op("dve", lambda e: e.tensor_copy(nlam[:, :], ops2[0][:, 0:1]), reads=[ops2[0]], writes=[nlam])
    gsc = load_featmajor_vec(p, cx, cx.b_subln_g, 1, "gsc", ops2[1])
    p.op("dve", lambda e: e.tensor_scalar(gsc[:, :], gsc[:, :], 1.0 - lam_init, None, ALU.mult), reads=[gsc], writes=[gsc])

    Qz = [p.sb("Qz%d" % i, [128, 8, 512], BF16) for i in range(2)]
    for i in range(2):
        p.op("pool", lambda e, i=i: e.memset(Qz[i][:, :, :], 0.0), writes=[Qz[i]])
    PT = [p.sb("PT%d" % i, [128, 512], BF16) for i in range(3)]
    rs = p.sb("rs", [128, 512], F32)
    rs2 = p.sb("rs2", [128, 512], F32)
    ots = [p.sb("ots%d" % i, [128, 512], F32) for i in range(2)]
    sms = [p.sb("sms%d" % i, [128, 512], F32) for i in range(2)]
    o1 = p.sb("o1", [128, 512], F32)
    osq = p.sb("osq", [128, 512], BF16)
    OT = p.sb("OT", [128, 8, 512], BF16)
    rb = [p.sb("rb%d" % i, [128, D], F32) for i in range(2)]
    cnt = [0]
    scale = 64.0 ** -0.5
    NQ = S // 512 if DBG_TILES is None else DBG_TILES // 4
    pic = [0]
    sq_ps = ops2[1]

    def load_q(Q):
        for hh in range(8):
            for i in range(2):
                p.dma(Qz[i][i * 64:(i + 1) * 64, hh, :], QT_d[hh, i * 64:(i + 1) * 64, Q * 512:(Q + 1) * 512],
                      writes=[Qz[i]])

    def epi1(h):
        p.op("dve", lambda e: e.reciprocal(rs[:, :], sms[0][:, :]), reads=[sms[0]], writes=[rs])
        p.op("dve", lambda e: e.tensor_tensor(o1[:, :], ots[0][:, :], rs[:, :], ALU.mult), reads=[ots[0], rs], writes=[o1])
        p.op("dve", lambda e: e.reciprocal(rs[:, :], sms[1][:, :]), reads=[sms[1]], writes=[rs])
        p.op("dve", lambda e: e.tensor_tensor(rs[:, :], ots[1][:, :], rs[:, :], ALU.mult), reads=[ots[1], rs], writes=[rs])
        p.op("dve", lambda e: e.scalar_tensor_tensor(o1[:, :], rs[:, :], nlam[:, 0:1], o1[:, :], ALU.mult, ALU.add),
             reads=[o1, rs, nlam], writes=[o1])
        p.op("dve", lambda e: e.tensor_tensor(osq[:, :], o1[:, :], o1[:, :], ALU.mult), reads=[o1], writes=[osq])

    def epi2(h):
        p.op("pe", lambda e: e.matmul(sq_ps[:, :], ones_b[:, :], osq[:, :], start=True, stop=True),
             reads=[ones_b, osq], writes=[sq_ps])
        p.op("dve", lambda e: e.tensor_scalar(rs2[:, :], sq_ps[:, :], 1.0 / 128, 1e-5, ALU.mult, ALU.add),
             reads=[sq_ps], writes=[rs2])
        p.op("act", lambda e: e.activation(rs2[:, :], rs2[:, :], AF.Ln), reads=[rs2], writes=[rs2])
        p.op("act", lambda e: e.activation(rs2[:, :], rs2[:, :], AF.Exp, scale=-0.5), reads=[rs2], writes=[rs2])
        p.op("dve", lambda e: e.scalar_tensor_tensor(OT[:, h, :], o1[:, :], gsc[:, 0:1], rs2[:, :], ALU.mult, ALU.mult),
             reads=[o1, gsc, rs2], writes=[OT])

    def attention(Q):
        nkt = 4 * Q + 4
        items = []
        hooks = {}
        for h in range(8):
            for kt in range(nkt):
                for i in range(2):
                    o_ps, s_ps = oT[i], sm[i]

                    def sa(h=h, i=i, kt=kt):
                        pi = pic[0]
                        pic[0] += 1
                        st_, pt = sT[pi % 2], PT[pi % 3]
                        diag = kt >= 4 * Q
                        p.op("pe", lambda e: e.matmul(
                            st_[:, :], KTh[h][:, kt * 128:(kt + 1) * 128], Qz[i][:, h, :],
                            start=True, stop=(not diag)), reads=[KTh[h], Qz[i]], writes=[st_])
                        if diag:
                            p.op("pe", lambda e: e.matmul(
                                st_[:, :], cx.ident_b[:, :], CM[:, kt - 4 * Q, :], start=False, stop=True),
                                reads=[CM, cx.ident_b_t], writes=[st_])
                        p.op("act", lambda e: e.activation(pt[:, :], st_[:, :], AF.Exp, scale=scale),
                             reads=[st_], writes=[pt])
                        return pt

                    def sb_(h=h, i=i, kt=kt, o_ps=o_ps, s_ps=s_ps, holder=None):
                        pt = holder[0]
                        p.op("pe", lambda e: e.matmul(
                            o_ps[:, :], Vg[kt // 4][:, kt % 4, h * 128:(h + 1) * 128], pt[:, :],
                            start=(kt == 0), stop=(kt == nkt - 1)),
                            reads=[Vg[kt // 4], pt], writes=[o_ps])
                        p.op("pe", lambda e: e.matmul(
                            s_ps[:, :], ones_b[:, :], pt[:, :], start=(kt == 0), stop=(kt == nkt - 1)),
                            reads=[ones_b, pt], writes=[s_ps])
                        if kt == nkt - 1:
                            p.op("dve", lambda e: e.tensor_copy(sms[i][:, :], s_ps[:, :]), reads=[s_ps], writes=[sms[i]])
                            p.op("dve", lambda e: e.tensor_copy(ots[i][:, :], o_ps[:, :]), reads=[o_ps], writes=[ots[i]])
                            if i == 1:
                                epi1(h)

                    holder = [None]
                    items.append(((lambda sa=sa, holder=holder: holder.__setitem__(0, sa())),
                                  (lambda sb_=sb_, holder=holder: sb_(holder=holder))))
            last = len(items) - 1
            hooks.setdefault(min(last + min(20, 2 * nkt - 2), 16 * nkt - 1), []).append(lambda h=h: epi2(h))
        run_skewed(items, 2, hooks)

    for Q in range(NQ):
        load_q(Q)
        attention(Q)
        out_proj_residual(p, cx, OT, 8, Wo, h_in, h_out, Q * 512, rb, [ops2[0], ops2[0]], cnt)
    p.end_phase()


def rglru_phase(p, cx, layer, h_in, h_out):
    p.begin_phase()
    load_consts(p, cx)
    psb = [p.ps("psb%d" % i, [128, 512], F32) for i in range(4)]
    psg = [p.ps("psg%d" % i, [128, 512], F32) for i in range(4)]
    Wi = p.sb("Wi", [128, 8, 2048], BF16)
    Wo = p.sb("Wo", [128, 8, D], BF16)
    Wa = p.sb("Wa", [128, 8, 128], BF16)
    Wx = p.sb("Wx", [128, 8, 128], BF16)
    stage = [p.sb("wst%d" % i, [128, 1024], F32) for i in range(2)]
    gT = load_featmajor_vec(p, cx, cx.norm_mix_g[layer, :], 8, "gT", psb[0])
    cw = [load_featmajor_vec(p, cx, cx.c_conv_w[k, :], 8, "cw%d" % k, psb[0]) for k in range(4)]
    cb = load_featmajor_vec(p, cx, cx.c_conv_b, 8, "cb", psb[0])
    ba = load_featmajor_vec(p, cx, cx.c_b_a, 8, "ba", psb[0])
    bx = load_featmajor_vec(p, cx, cx.c_b_x, 8, "bx", psb[0])
    lamT = load_featmajor_vec(p, cx, cx.c_lambda, 8, "lamT", psb[0])
    cf = p.sb("cf", [128, 8], F32)
    cf2 = p.sb("cf2", [128, 8], F32)
    p.op("act", lambda e: e.activation(cf[:, :], lamT[:, :], AF.Exp, scale=-1.0), reads=[lamT], writes=[cf])
    p.op("dve", lambda e: e.tensor_scalar(cf[:, :], cf[:, :], 1.0, None, ALU.add), reads=[cf], writes=[cf])
    p.op("act", lambda e: e.activation(cf[:, :], cf[:, :], AF.Ln), reads=[cf], writes=[cf])
    p.op("dve", lambda e: e.tensor_scalar(cf2[:, :], cf[:, :], -16.0, None, ALU.mult), reads=[cf], writes=[cf2])
    p.op("dve", lambda e: e.tensor_scalar(cf[:, :], cf[:, :], -8.0, None, ALU.mult), reads=[cf], writes=[cf])
    load_weight_bf16(p, cx.c_w_in, D, 2048, Wi, stage, gT=gT, col_chunk=1024)
    load_weight_bf16(p, cx.c_w_out, D, D, Wo, stage, col_chunk=1024)
    for g in range(8):
        st = stage[g % 2]
        p.dma(st[:, 0:128], cx.c_w_a[g, :, :], writes=[st])
        p.op("dve", lambda e, st=st, g=g: e.tensor_copy(Wa[:, g, :], st[:, 0:128]), reads=[st], writes=[Wa])
        p.dma(st[:, 128:256], cx.c_w_x[g, :, :], writes=[st])
        p.op("dve", lambda e, st=st, g=g: e.tensor_copy(Wx[:, g, :], st[:, 128:256]), reads=[st], writes=[Wx])

    halo = p.sb("halo", [128, 8, 3], F32)
    p.op("pool", lambda e: e.memset(halo[:, :, :], 0.0), writes=[halo])
    carry = p.sb("carry", [128, 8], F32)
    p.op("pool", lambda e: e.memset(carry[:, :], 0.0), writes=[carry])
    hn = [p.sb("hn%d" % i, [128, D], F32) for i in range(2)]
    rb = [p.sb("rb%d" % i, [128, D], F32) for i in range(2)]
    xn = [p.sb("xn%d" % i, [128, D], BF16) for i in range(2)]
    ss = [p.sb("ss%d" % i, [128, 1], F32) for i in range(2)]
    rstd = [p.sb("rstd%d" % i, [128, 1], F32) for i in range(2)]
    XnT = [p.sb("XnT%d" % i, [128, 8, 512], BF16) for i in range(2)]
    YT = p.sb("YT", [128, 8, 512], BF16)
    R = [p.sb("R%d" % i, [128, 515], F32) for i in range(4)]
    cv = [p.sb("cv%d" % i, [128, 512], F32) for i in range(4)]
    cvb = [p.sb("cvb%d" % i, [128, 512], BF16) for i in range(4)]
    rr = [p.sb("rr%d" % i, [128, 512], F32) for i in range(4)]
    ii = [p.sb("ii%d" % i, [128, 512], F32) for i in range(4)]
    aa = [p.sb("aa%d" % i, [128, 512], F32) for i in range(4)]
    mm = [p.sb("mm%d" % i, [128, 512], F32) for i in range(4)]
    hs = [p.sb("hs%d" % i, [128, 512], F32) for i in range(4)]
    gq = [p.sb("gq%d" % i, [128, 512], F32) for i in range(4)]
    gsb = [p.sb("gsb%d" % i, [128, 512], F32) for i in range(4)]
    cnt = [0]
    NTT = S // 512 if DBG_TILES is None else DBG_TILES // 4

    def prep(tt):
        X = XnT[tt % 2]
        trb = psb[2].h[:, :].bitcast(BF16)
        for s in range(4):
            i = tt * 4 + s
            t = hn[i % 2]
            r0 = i * 128
            p.dma(t[:, :], h_in[r0:r0 + 128, :], writes=[t])
            k = i % 2
            norm_tile(p, t, xn[k], ss[k], rstd[k], xn[k])
            for j in range(8):
                p.op("pe", lambda e, j=j, k=k: e.transpose(trb[:, j * 128:(j + 1) * 128],
                                                          xn[k][:, j * 128:(j + 1) * 128], cx.ident_b[:, :]),
                     reads=[xn[k], cx.ident_b_t], writes=[psb[2]])
            p.op("dve", lambda e, s=s, X=X: e.tensor_copy(
                X[:, :, s * 128:(s + 1) * 128], trb[:, 0:1024].rearrange("p (j t) -> p j t", t=128)),
                reads=[psb[2]], writes=[X])

    def chunk_gen(tt, c):
        X = XnT[tt % 2]
        b = c % 4
        gps, rps = psb[b % 2], psb[2 + b % 2]
        pa, px = psg[2 * (b % 2)], psg[2 * (b % 2) + 1]
        gs = gsb[b]
        for kc in range(8):
            p.op("pe", lambda e, kc=kc: e.matmul(
                gps[:, :], Wi[:, kc, c * 128:(c + 1) * 128], X[:, kc, :], start=(kc == 0), stop=(kc == 7)),
                reads=[Wi, X], writes=[gps])
        for kc in range(8):
            p.op("pe", lambda e, kc=kc: e.matmul(
                rps[:, :], Wi[:, kc, 1024 + c * 128:1024 + (c + 1) * 128], X[:, kc, :],
                start=(kc == 0), stop=(kc == 7)),
                reads=[Wi, X], writes=[rps])
        r_ = R[b]
        p.op("pool", lambda e: e.tensor_copy(r_[:, 0:3], halo[:, c, :]), reads=[halo], writes=[r_])
        p.op("act", lambda e: e.activation(r_[:, 3:515], rps[:, :], AF.Copy), reads=[rps], writes=[r_])
        p.op("act", lambda e: e.activation(gs[:, :], gps[:, :], AF.Copy), reads=[gps], writes=[gs])
        p.op("pool", lambda e: e.tensor_copy(halo[:, c, :], r_[:, 512:515]), reads=[r_], writes=[halo])
        yield
        p.op("act", lambda e: e.activation(
            cv[b][:, :], r_[:, 0:512], AF.Identity, bias=cb[:, c:c + 1], scale=cw[0][:, c:c + 1]),
            reads=[r_, cb, cw[0]], writes=[cv[b]])
        yield
        for k in range(1, 4):
            p.op("dve", lambda e, k=k: e.scalar_tensor_tensor(
                cv[b][:, :], r_[:, k:k + 512], cw[k][:, c:c + 1], cv[b][:, :], ALU.mult, ALU.add),
                reads=[r_, cw[k], cv[b]], writes=[cv[b]])
            yield
        p.op("act", lambda e: e.activation(cvb[b][:, :], cv[b][:, :], AF.Copy), reads=[cv[b]], writes=[cvb[b]])
        yield
        p.op("pe", lambda e: e.matmul(pa[:, :], Wa[:, c, :], cvb[b][:, :], start=True, stop=True),
             reads=[Wa, cvb[b]], writes=[pa])
        p.op("pe", lambda e: e.matmul(px[:, :], Wx[:, c, :], cvb[b][:, :], start=True, stop=True),
             reads=[Wx, cvb[b]], writes=[px])
        p.op("act", lambda e: e.activation(rr[b][:, :], pa[:, :], AF.Sigmoid, bias=ba[:, c:c + 1]),
             reads=[pa, ba], writes=[rr[b]])
        p.op("act", lambda e: e.activation(ii[b][:, :], px[:, :], AF.Sigmoid, bias=bx[:, c:c + 1]),
             reads=[px, bx], writes=[ii[b]])
        yield
        p.op("act", lambda e: e.activation(aa[b][:, :], rr[b][:, :], AF.Exp, scale=cf[:, c:c + 1]),
             reads=[rr[b], cf], writes=[aa[b]])
        p.op("act", lambda e: e.activation(mm[b][:, :], rr[b][:, :], AF.Exp, scale=cf2[:, c:c + 1]),
             reads=[rr[b], cf2], writes=[mm[b]])
        yield
        p.op("dve", lambda e: e.tensor_scalar(mm[b][:, :], mm[b][:, :], -1.0, 1.0, ALU.mult, ALU.add),
             reads=[mm[b]], writes=[mm[b]])
        p.op("dve", lambda e: e.tensor_tensor(ii[b][:, :], ii[b][:, :], cv[b][:, :], ALU.mult),
             reads=[ii[b], cv[b]], writes=[ii[b]])
        yield
        p.op("act", lambda e: e.activation(mm[b][:, :], mm[b][:, :], AF.Sqrt), reads=[mm[b]], writes=[mm[b]])
        p.op("act", lambda e: e.activation(gq[b][:, :], gs[:, :], AF.Square), reads=[gs], writes=[gq[b]])
        yield
        p.op("dve", lambda e: e.tensor_tensor(mm[b][:, :], mm[b][:, :], ii[b][:, :], ALU.mult),
             reads=[mm[b], ii[b]], writes=[mm[b]])
        yield
        p.op("dve", lambda e: e.tensor_tensor_scan(
            hs[b][:, :], aa[b][:, :], mm[b][:, :], carry[:, c:c + 1], ALU.mult, ALU.add),
            reads=[aa[b], mm[b], carry], writes=[hs[b]])
        p.op("pool", lambda e: e.tensor_copy(carry[:, c:c + 1], hs[b][:, 511:512]),
             reads=[hs[b]], writes=[carry])
        yield
        p.op("dve", lambda e: e.tensor_scalar(gq[b][:, :], gq[b][:, :], 0.044715, 1.0, ALU.mult, ALU.add),
             reads=[gq[b]], writes=[gq[b]])
        yield
        p.op("dve", lambda e: e.tensor_tensor(gq[b][:, :], gq[b][:, :], gs[:, :], ALU.mult),
             reads=[gq[b], gs], writes=[gq[b]])
        yield
        p.op("act", lambda e: e.activation(gq[b][:, :], gq[b][:, :], AF.Sigmoid, scale=1.5957691216057308),
             reads=[gq[b]], writes=[gq[b]])
        yield
        p.op("dve", lambda e: e.tensor_tensor(gq[b][:, :], gq[b][:, :], gs[:, :], ALU.mult),
             reads=[gq[b], gs], writes=[gq[b]])
        yield
        p.op("dve", lambda e: e.tensor_tensor(YT[:, c, :], hs[b][:, :], gq[b][:, :], ALU.mult),
             reads=[hs[b], gq[b]], writes=[YT])

    def interleave(gens, stagger=4):
        pending = list(gens)
        live = []
        rnd = 0
        while pending or live:
            if pending and rnd % stagger == 0:
                live.append(pending.pop(0))
            for g in list(live):
                try:
                    next(g)
                except StopIteration:
                    live.remove(g)
            rnd += 1

    prep(0)
    for tt in range(NTT):
        for c in range(0, 8, 4):
            interleave([chunk_gen(tt, c + j) for j in range(4)])
            if c == 0 and tt + 1 < NTT:
                prep(tt + 1)
        out_proj_residual(p, cx, YT, 8, Wo, h_in, h_out, tt * 512, rb, [psb[0], psb[1]], cnt)
    p.end_phase()


def ret_proj_phase(p, cx, layer, h_in, QT_d, KT_d, KZ_d, V2_d, SG_d):
    def load_cs(p, st, t):
        k = t % 2
        p.dma(st.cs[k][:, 0, :], cx.d_cos_d[:, t, :], writes=[st.cs[k]])
        p.dma(st.cs[k][:, 1, :], cx.d_sin_d[:, t, :], writes=[st.cs[k]])

    def setup(p, ps_tr):
        st = Ctx()
        st.cs = [p.sb("cs%d" % i, [128, 2, 256], F32) for i in range(2)]
        st.qk = [p.sb("qk%d" % i, [128, 2048], BF16) for i in range(2)]
        st.kz = [p.sb("kz%d" % i, [128, 1024], BF16) for i in range(2)]
        st.vb = [p.sb("vb%d" % i, [128, 2048], BF16) for i in range(2)]
        st.sg = [p.sb("sg%d" % i, [128, 2048], BF16) for i in range(2)]
        st.tmp = [p.sb("rt%d" % i, [128, 256], F32) for i in range(4)]
        st.QKT = [p.sb("QKT%d" % i, [128, 16, 512], BF16) for i in range(2)]
        st.ZT = p.sb("ZT", [128, 4], F32)
        p.dma(st.ZT[:, :], cx.d_zeta[:, :], writes=[st.ZT])
        load_cs(p, st, 0)
        return st

    def epi(p, st, t, gb, bank):
        k = t % 2
        if gb < 4:
            rope_bank(p, bank, st.qk[k], gb * 512, st.cs[k][:, 0, :], st.cs[k][:, 1, :], [st.cs[k]], 2, 256, 128, st.tmp)
            if gb >= 2:
                c0 = gb * 512
                p.op("act", lambda e: e.activation(st.qk[k][:, c0:c0 + 512], st.qk[k][:, c0:c0 + 512], AF.Copy,
                                                   scale=1.0 / 16),
                     reads=[st.qk[k]], writes=[st.qk[k]])
                for j in range(2):
                    hh = (gb - 2) * 2 + j
                    p.op("act", lambda e, j=j, hh=hh: e.activation(
                        st.kz[k][:, hh * 256:(hh + 1) * 256], st.qk[k][:, c0 + j * 256:c0 + (j + 1) * 256],
                        AF.Copy, scale=st.ZT[:, hh:hh + 1]),
                        reads=[st.qk[k], st.ZT], writes=[st.kz[k]])
        elif gb < 8:
            p.op("act", lambda e: e.activation(st.vb[k][:, (gb - 4) * 512:(gb - 3) * 512], bank[:, :], AF.Copy),
                 reads=[bank], writes=[st.vb[k]])
        else:
            p.op("act", lambda e: e.activation(st.sg[k][:, (gb - 8) * 512:(gb - 7) * 512], bank[:, :], AF.Silu),
                 reads=[bank], writes=[st.sg[k]])

    def tile_done(p, st, t, ps_tr):
        k = t % 2
        if t + 1 < NT:
            load_cs(p, st, t + 1)
        p.dma(KZ_d[t * 128:(t + 1) * 128, :], st.kz[k][:, :], reads=[st.kz[k]])
        p.dma(V2_d[t * 128:(t + 1) * 128, :], st.vb[k][:, :], reads=[st.vb[k]])
        p.dma(SG_d[t * 128:(t + 1) * 128, :], st.sg[k][:, :], reads=[st.sg[k]])
        g = (t // 4) % 2
        transpose_to(p, cx, st.qk[k], 16, ps_tr, st.QKT[g], (t % 4) * 128, evac_eng="act")
        if t % 4 == 3:
            t0 = (t // 4) * 512
            for hh in range(8):
                p.dma(QT_d[hh, :, t0:t0 + 512], st.QKT[g][:, hh, :], reads=[st.QKT[g]])
                p.dma(KT_d[hh, :, t0:t0 + 512], st.QKT[g][:, 8 + hh, :], reads=[st.QKT[g]])

    proj_phase(p, cx, layer, cx.d_w_in, 6144, h_in, setup, epi, tile_done)


def ret_core_phase(p, cx, h_in, h_out, QT_d, KT_d, KZ_d, V2_d, SG_d):
    p.begin_phase()
    load_consts(p, cx)
    psI = p.ps("psI", [128, 512], F32)
    psA = [p.ps("psA%d" % i, [128, 512], F32) for i in range(2)]
    psB = [p.ps("psB%d" % i, [128, 512], F32) for i in range(2)]
    psS = [p.ps("psS%d" % i, [128, 512], F32) for i in range(2)]
    ops1 = p.ps("ops1", [128, 512], F32)
    ops_b = ops1.h[:, :].bitcast(BF16)
    Wo = p.sb("Wo", [128, 16, D], BF16)
    stage = [p.sb("wst%d" % i, [128, 1024], F32) for i in range(2)]
    gnT = load_featmajor_vec(p, cx, cx.d_gn_g, 16, "gnT", psI)
    load_weight_bf16(p, cx.d_w_out, 2048, D, Wo, stage, gT=gnT, col_chunk=1024)
    DT = p.sb("DT", [128, 4, 128], F32)
    p.dma(DT[:, :, :], cx.d_decay[:, :, :], writes=[DT])
    XI = p.sb("XI", [128, 4], F32)
    p.dma(XI[:, :], cx.d_xi[:, :], writes=[XI])
    state_f = p.sb("state_f", [128, 8, 512], F32)
    state_b = p.sb("state_b", [128, 8, 512], BF16)
    p.op("pool", lambda e: e.memset(state_f[:, :, :], 0.0), writes=[state_f])
    p.op("pool", lambda e: e.memset(state_b[:, :, :], 0.0), writes=[state_b])
    QTs = [p.sb("QTs%d" % i, [128, 8, 512], BF16) for i in range(2)]
    KTs = [p.sb("KTs%d" % i, [128, 8, 512], BF16) for i in range(2)]
    kz = [p.sb("kz%d" % i, [128, 1024], BF16) for i in range(2)]
    vc = [p.sb("vc%d" % i, [128, 2048], BF16) for i in range(2)]
    sgc = [p.sb("sgc%d" % i, [128, 2048], BF16) for i in range(3)]
    inT = [p.sb("inT%d" % i, [128, 4, 128], BF16) for i in range(2)]
    oi = [p.sb("oi%d" % i, [128, 512], F32) for i in range(2)]
    oo = [p.sb("oo%d" % i, [128, 4, 512], F32) for i in range(2)]
    bst = [p.sb("bst%d" % i, [128, 4, 6], F32) for i in range(2)]
    mv = [p.sb("mv%d" % i, [128, 4, 2], F32) for i in range(2)]
    rsd = [p.sb("rsd%d" % i, [128, 4], F32) for i in range(2)]
    y = [p.sb("y%d" % i, [128, 2048], BF16) for i in range(2)]
    yT = [p.sb("yT%d" % i, [128, 16, 128], BF16) for i in range(2)]
    rb = [p.sb("rb%d" % i, [128, D], F32) for i in range(2)]
    gc = [float((1.0 - 2.0 ** (-5 - h)) ** 128) for h in range(4)]
    NC_ = NT if DBG_TILES is None else DBG_TILES

    def load_big(cq):
        for hh in range(8):
            p.dma(QTs[cq % 2][:, hh, :], QT_d[hh, :, cq * 512:(cq + 1) * 512], writes=[QTs[cq % 2]])
            p.dma(KTs[cq % 2][:, hh, :], KT_d[hh, :, cq * 512:(cq + 1) * 512], writes=[KTs[cq % 2]])

    def load_chunk(c):
        k = c % 2
        p.dma(kz[k][:, :], KZ_d[c * 128:(c + 1) * 128, :], writes=[kz[k]])
        p.dma(vc[k][:, :], V2_d[c * 128:(c + 1) * 128, :], writes=[vc[k]])
        p.dma(sgc[c % 3][:, :], SG_d[c * 128:(c + 1) * 128, :], writes=[sgc[c % 3]])

    def chunk(c):
        k = c % 2
        qt, kt_ = QTs[(c // 4) % 2], KTs[(c // 4) % 2]
        tc0 = (c % 4) * 128
        for h in range(4):
            for dk in range(2):
                p.op("pe", lambda e, h=h, dk=dk: e.matmul(
                    psI[:, h * 128:(h + 1) * 128], kt_[:, h * 2 + dk, tc0:tc0 + 128], qt[:, h * 2 + dk, tc0:tc0 + 128],
                    start=(dk == 0), stop=(dk == 1)), reads=[qt, kt_], writes=[psI])
        p.op("dve", lambda e: e.tensor_tensor(inT[k][:, :, :], psI[:, :].rearrange("p (h q) -> p h q", q=128),
                                              DT[:, :, :], ALU.mult), reads=[psI, DT], writes=[inT[k]])
        for h in range(4):
            b = h % 2
            for dk in range(2):
                p.op("pe", lambda e, h=h, dk=dk: e.matmul(
                    psS[dk][:, :], kz[k][:, (h * 2 + dk) * 128:(h * 2 + dk + 1) * 128], vc[k][:, h * 512:(h + 1) * 512],
                    start=True, stop=True), reads=[kz[k], vc[k]], writes=[psS[dk]])
            for dk in range(2):
                p.op("pe", lambda e, h=h, dk=dk, b=b: e.matmul(
                    psB[b][:, :], qt[:, h * 2 + dk, tc0:tc0 + 128], state_b[:, h * 2 + dk, :],
                    start=(dk == 0), stop=(dk == 1)), reads=[qt, state_b], writes=[psB[b]])
            p.op("pe", lambda e, h=h, b=b: e.matmul(
                psA[b][:, :], inT[k][:, h, :], vc[k][:, h * 512:(h + 1) * 512], start=True, stop=True),
                reads=[inT[k], vc[k]], writes=[psA[b]])
            for dk in range(2):
                p.op("dve", lambda e, h=h, dk=dk: e.scalar_tensor_tensor(
                    state_f[:, h * 2 + dk, :], state_f[:, h * 2 + dk, :], gc[h], psS[dk][:, :], ALU.mult, ALU.add),
                    reads=[state_f, psS[dk]], writes=[state_f])
            p.op("act", lambda e, b=b: e.activation(oi[b][:, :], psA[b][:, :], AF.Copy), reads=[psA[b]], writes=[oi[b]])
            for dk in range(2):
                p.op("act", lambda e, h=h, dk=dk: e.activation(state_b[:, h * 2 + dk, :], state_f[:, h * 2 + dk, :], AF.Copy),
                     reads=[state_f], writes=[state_b])
            p.op("dve", lambda e, h=h, b=b: e.scalar_tensor_tensor(
                oo[k][:, h, :], psB[b][:, :], XI[:, h:h + 1], oi[b][:, :], ALU.mult, ALU.add),
                reads=[psB[b], XI, oi[b]], writes=[oo[k]])
            p.op("dve", lambda e, h=h: e.bn_stats(bst[k][:, h, :], oo[k][:, h, :]), reads=[oo[k]], writes=[bst[k]])
            p.op("dve", lambda e, h=h: e.bn_aggr(mv[k][:, h, :], bst[k][:, h, :]), reads=[bst[k]], writes=[mv[k]])
    def chunk2(c):
        k = c % 2
        p.op("dve", lambda e: e.tensor_scalar(rsd[k][:, :], mv[k][:, :, 1], 1e-5, None, ALU.add),
             reads=[mv[k]], writes=[rsd[k]])
        p.op("act", lambda e: e.activation(rsd[k][:, :], rsd[k][:, :], AF.Sqrt), reads=[rsd[k]], writes=[rsd[k]])
        p.op("dve", lambda e: e.reciprocal(rsd[k][:, :], rsd[k][:, :]), reads=[rsd[k]], writes=[rsd[k]])
        for h in range(4):
            p.op("dve", lambda e, h=h: e.tensor_scalar(oo[k][:, h, :], oo[k][:, h, :], mv[k][:, h, 0:1], rsd[k][:, h:h + 1],
                                                      ALU.subtract, ALU.mult),
                 reads=[oo[k], mv[k], rsd[k]], writes=[oo[k]])
        oo2 = oo[k].h[:, :, :].rearrange("p h d -> p (h d)")
        p.op("dve", lambda e: e.tensor_tensor(y[k][:, :], oo2, sgc[c % 3][:, :], ALU.mult),
             reads=[oo[k], sgc[c % 3]], writes=[y[k]])
    def chunk2b(c):
        k = c % 2
        for j0 in range(0, 16, 8):
            for j in range(8):
                p.op("pe", lambda e, j=j, j0=j0: e.transpose(ops_b[:, j * 128:(j + 1) * 128],
                                                           y[k][:, (j0 + j) * 128:(j0 + j + 1) * 128], cx.ident_b[:, :]),
                     reads=[y[k], cx.ident_b_t], writes=[ops1])
            p.op("act", lambda e, j0=j0: e.activation(
                yT[k][:, j0:j0 + 8, :], ops_b[:, 0:1024].rearrange("p (j t) -> p j t", t=128), AF.Copy),
                reads=[ops1], writes=[yT[k]])
        hbt = rb[k]
        p.dma(hbt[:, :], h_in[c * 128:(c + 1) * 128, :], writes=[hbt])
        for half in range(2):
            for kc in range(16):
                p.op("pe", lambda e, kc=kc, half=half: e.matmul(
                    ops1[:, :], yT[k][:, kc, :], Wo[:, kc, half * 512:(half + 1) * 512],
                    start=(kc == 0), stop=(kc == 15)), reads=[yT[k], Wo], writes=[ops1])
            p.op("dve", lambda e, half=half: e.tensor_tensor(
                hbt[:, half * 512:(half + 1) * 512], hbt[:, half * 512:(half + 1) * 512], ops1[:, :], ALU.add),
                reads=[hbt, ops1], writes=[hbt])
        p.dma(h_out[c * 128:(c + 1) * 128, :], hbt[:, :], reads=[hbt])

    load_big(0)
    load_chunk(0)
    for c in range(NC_):
        if c % 4 == 0 and (c // 4 + 1) * 4 < NC_:
            load_big(c // 4 + 1)
        if c + 1 < NC_:
            load_chunk(c + 1)
        if c >= 1:
            chunk2(c - 1)
        chunk(c)
        if c >= 1:
            chunk2b(c - 1)
    chunk2(NC_ - 1)
    chunk2b(NC_ - 1)
    p.end_phase()


def final_phase(p, cx, h_in, out):
    p.begin_phase()
    gb = p.sb("gb", [128, D], F32)
    p.dma(gb[:, :], cx.norm_final_g.rearrange("(o d) -> o d", o=1).broadcast_to([128, D]), writes=[gb])
    hb = [p.sb("hb%d" % i, [128, D], F32) for i in range(3)]
    ob = [p.sb("ob%d" % i, [128, D], F32) for i in range(2)]
    junk = p.sb("junk", [128, D], BF16)
    ss = [p.sb("ss%d" % i, [128, 1], F32) for i in range(2)]
    rstd = [p.sb("rstd%d" % i, [128, 1], F32) for i in range(2)]
    for t in range(NT):
        hbt = hb[t % 3]
        k = t % 2
        p.dma(hbt[:, :], h_in[t * 128:(t + 1) * 128, :], writes=[hbt])
        p.op("act", lambda e, hbt=hbt, k=k: e.activation(junk[:, :], hbt[:, :], AF.Square, accum_out=ss[k][:, 0:1]),
             reads=[hbt], writes=[junk, ss[k]])
        p.op("dve", lambda e, k=k: e.tensor_scalar(rstd[k][:, 0:1], ss[k][:, 0:1], 1.0 / D, RMS_EPS, ALU.mult, ALU.add),
             reads=[ss[k]], writes=[rstd[k]])
        p.op("act", lambda e, k=k: e.activation(rstd[k][:, 0:1], rstd[k][:, 0:1], AF.Sqrt), reads=[rstd[k]], writes=[rstd[k]])
        p.op("dve", lambda e, k=k: e.reciprocal(rstd[k][:, 0:1], rstd[k][:, 0:1]), reads=[rstd[k]], writes=[rstd[k]])
        p.op("dve", lambda e, hbt=hbt, k=k: e.scalar_tensor_tensor(
            ob[k][:, :], hbt[:, :], rstd[k][:, 0:1], gb[:, :], ALU.mult, ALU.mult),
            reads=[hbt, rstd[k], gb], writes=[ob[k]])
        p.dma(out[t * 128:(t + 1) * 128, :], ob[k][:, :], reads=[ob[k]])
    p.end_phase()


INPUT_SPECS = [
    ("x", [S, D]), ("norm_mix_g", [4, D]), ("norm_ffn_g", [4, D]), ("norm_final_g", [D]),
    ("a_w_in", [D, 3072]), ("a_w_out", [D, D]),
    ("b_w_in", [D, 3072]), ("b_w_out", [D, D]), ("b_lam_q1", [64]), ("b_lam_k1", [64]),
    ("b_lam_q2", [64]), ("b_lam_k2", [64]), ("b_subln_g", [128]),
    ("c_w_in", [D, 2048]), ("c_conv_w", [4, D]), ("c_conv_b", [D]), ("c_w_a", [8, 128, 128]), ("c_b_a", [D]),
    ("c_w_x", [8, 128, 128]), ("c_b_x", [D]), ("c_lambda", [D]), ("c_w_out", [D, D]),
    ("d_w_in", [D, 6144]), ("d_gn_g", [2048]), ("d_w_out", [2048, D]),
    ("ffn_w_in", [4, D, 2 * D_FF]), ("ffn_conv_w", [4, 3, D_FF]), ("ffn_conv_b", [4, D_FF]),
    ("ffn_w_out", [4, D_FF, D]),
]


def rope_np(half_dim_total, theta, rep):
    inv = theta ** (-np.arange(0, half_dim_total, 2, dtype=np.float32) / np.float32(half_dim_total))
    ang = np.arange(S, dtype=np.float32)[:, None] * inv[None, :].astype(np.float32)
    lay = lambda a: np.ascontiguousarray(
        np.tile(a.astype(np.float32), (1, rep)).reshape(NT, 128, -1).transpose(1, 0, 2))
    return lay(np.cos(ang)), lay(np.sin(ang))


def host_consts():
    bf = ml_dtypes.bfloat16
    c = {}
    c["k_ident_f"] = np.eye(128, dtype=np.float32)
    c["k_ident_b"] = np.eye(128, dtype=np.float32).astype(bf)
    c["k_cos_a"], c["k_sin_a"] = rope_np(32, 500000.0, 4)
    c["k_cos_b"], c["k_sin_b"] = rope_np(16, 500000.0, 8)
    c["k_cos_d"], c["k_sin_d"] = rope_np(256, 10000.0, 2)
    kk = np.arange(128)[:, None, None] + 128 * np.arange(4)[None, :, None]
    qq = np.arange(512)[None, None, :]
    c["k_cmask"] = np.where(kk <= qq, 0.0, NEG).astype(np.float32).astype(bf)
    en = np.zeros((16, 16, 128), np.float32)
    for n in range(16):
        en[n, n, :] = 1.0
    c["k_en"] = en.astype(bf)
    qb = np.arange(16)[:, None]
    n = np.arange(16)[None, :]
    pb2 = np.where(n < qb, 0.0, np.where(n == qb, 1e30, -2e30)).astype(np.float32)
    c["k_pb2"] = np.ascontiguousarray(np.broadcast_to(pb2[None], (128, 16, 16)))
    lg = np.log1p(-np.exp2(-5.0 - np.arange(4, dtype=np.float64)))
    pos = np.arange(128, dtype=np.float64)
    kk_ = pos[:, None, None]
    qq_ = pos[None, None, :]
    dec = np.where(qq_ >= kk_, np.exp(np.maximum(qq_ - kk_, 0.0) * lg[None, :, None]), 0.0)
    c["k_decay"] = dec.astype(np.float32)
    c["k_xi"] = np.exp((pos[:, None] + 1.0) * lg[None, :]).astype(np.float32)
    c["k_zeta"] = np.exp((127.0 - pos[:, None]) * lg[None, :]).astype(np.float32)
    return c


CONST_SPECS = [("k_ident_f", [128, 128], F32), ("k_ident_b", [128, 128], BF16),
               ("k_cos_a", [128, NT, 64], F32), ("k_sin_a", [128, NT, 64], F32),
               ("k_cos_b", [128, NT, 64], F32), ("k_sin_b", [128, NT, 64], F32),
               ("k_cos_d", [128, NT, 256], F32), ("k_sin_d", [128, NT, 256], F32),
               ("k_cmask", [128, 4, 512], BF16), ("k_en", [16, 16, 128], BF16), ("k_pb2", [128, 16, 16], F32),
               ("k_decay", [128, 4, 128], F32), ("k_xi", [128, 4], F32), ("k_zeta", [128, 4], F32)]


def build_program(stages):
    nc = bass.Bass("TRN2", target_bir_lowering=False)
    cx = Ctx()
    for name, shape in INPUT_SPECS:
        setattr(cx, name, nc.dram_tensor(name, list(shape), F32, kind="ExternalInput").ap())
    for name, shape, dt in CONST_SPECS:
        setattr(cx, "d_" + name[2:], nc.dram_tensor(name, list(shape), dt, kind="ExternalInput").ap())
    out = nc.dram_tensor("out", [S, D], F32, kind="ExternalOutput").ap()
    hA = nc.dram_tensor("hA", [S, D], F32).ap()
    QT_d = nc.dram_tensor("QT_d", [8, 128, S], BF16).ap()
    KT_d = nc.dram_tensor("KT_d", [8, 128, S], BF16).ap()
    V_d = nc.dram_tensor("V_d", [S, D], BF16).ap()
    V2_d = nc.dram_tensor("V2_d", [S, 2048], BF16).ap()
    SG_d = nc.dram_tensor("SG_d", [S, 2048], BF16).ap()
    p = Prog(nc)
    cur = cx.x
    for st in stages:
        kind = st[0]
        if kind == "ffn":
            ffn_phase(p, cx, st[1], cur, hA)
            cur = hA
        elif kind == "diff":
            qkv_proj_phase(p, cx, 1, cx.b_w_in, cur, cx.d_cos_b, cx.d_sin_b, 8, 64, 8, QT_d, KT_d, V_d)
            diff_attn_phase(p, cx, cur, hA, QT_d, KT_d, V_d)
            cur = hA
        elif kind == "rglru":
            rglru_phase(p, cx, 2, cur, hA)
            cur = hA
        elif kind == "ret":
            ret_proj_phase(p, cx, 3, cur, QT_d, KT_d, V_d, V2_d, SG_d)
            ret_core_phase(p, cx, cur, hA, QT_d, KT_d, V_d, V2_d, SG_d)
            cur = hA
        elif kind == "moba_a":
            qkv_proj_phase(p, cx, 0, cx.a_w_in, cur, cx.d_cos_a, cx.d_sin_a, 4, 128, 16, QT_d, KT_d, V_d)
        elif kind == "moba":
            qkv_proj_phase(p, cx, 0, cx.a_w_in, cur, cx.d_cos_a, cx.d_sin_a, 4, 128, 16, QT_d, KT_d, V_d)
            moba_attn_phase(p, cx, cur, hA, QT_d, KT_d, V_d)
            cur = hA
        elif kind == "final":
            final_phase(p, cx, cur, out)
        elif kind == "copyout":
            copy_phase(p, cx, cur, out)
    p.es.close()
    return nc


def copy_phase(p, cx, h_in, out):
    p.begin_phase()
    hb = [p.sb("hb%d" % i, [128, D], F32) for i in range(4)]
    for t in range(NT):
        hbt = hb[t % 4]
        p.dma(hbt[:, :], h_in[t * 128:(t + 1) * 128, :], writes=[hbt])
        p.dma(out[t * 128:(t + 1) * 128, :], hbt[:, :], reads=[hbt])
    p.end_phase()


FULL_STAGES = [("moba",), ("ffn", 0), ("diff",), ("ffn", 1), ("rglru",), ("ffn", 2), ("ret",), ("ffn", 3), ("final",)]


def run(inputs, stages, trace=False):
    nc = build_program(stages)
    consts = host_consts()
    in_maps = []
    for b in range(8):
        m = {}
        for name, shape in INPUT_SPECS:
            a = np.asarray(inputs[name])
            if name == "x":
                a = a[b]
            elif name.startswith(("a_", "b_", "c_", "d_")):
                a = a[0]
            m[name] = np.ascontiguousarray(a.reshape(shape).astype(np.float32, copy=False))
        m.update(consts)
        in_maps.append(m)
    res = run_bass_kernel_spmd(nc, in_maps, core_ids=list(range(8)), trace=trace)
    return np.stack([np.asarray(r["out"]) for r in res.results], axis=0), res


def kernel(**inputs):
    out, _ = run(inputs, FULL_STAGES)
    return out.astype(np.float32)
```

```python
import math
from contextlib import ExitStack

import numpy as np
import ml_dtypes
import concourse.bass as bass
import concourse.mybir as mybir
from concourse.bass_utils import run_bass_kernel_spmd

F32 = mybir.dt.float32
BF16 = mybir.dt.bfloat16
AF = mybir.ActivationFunctionType
ALU = mybir.AluOpType
AX = mybir.AxisListType

S = 4096
D = 1024
NT = S // 128
D_FF = 2816
NFC = D_FF // 128
RMS_EPS = 1e-6
NEG = -30000.0
DBG_TILES = None

ENGS = ("pe", "act", "dve", "pool", "sp")
N_DMA_SEMS = 24


class T:
    def __init__(self, name, handle=None, psum=False):
        self.name = name
        self.h = handle
        self.psum = psum
        self.writer = None
        self.readers = []

    def __getitem__(self, idx):
        return self.h[idx]


class Op:
    __slots__ = ("eng", "fn", "deps", "is_dma", "signal", "idx", "sem", "val")

    def __init__(self, eng, fn, is_dma=False):
        self.eng = eng
        self.fn = fn
        self.deps = []
        self.is_dma = is_dma
        self.signal = False
        self.idx = None
        self.sem = None
        self.val = None


class Prog:
    def __init__(self, nc):
        self.nc = nc
        self.es = ExitStack()
        self.sems = {e: self.es.enter_context(nc.semaphore("sem_" + e)) for e in ENGS}
        self.dma_sems = [self.es.enter_context(nc.semaphore("dsem%d" % i)) for i in range(N_DMA_SEMS)]
        self.ops = []
        self.phase_es = None
        self.n_phase = 0

    def begin_phase(self):
        self.phase_es = ExitStack()
        self.ops = []

    def sb(self, name, shape, dtype=F32):
        h = self.phase_es.enter_context(self.nc.sbuf_tensor("%s_p%d" % (name, self.n_phase), list(shape), dtype))
        return T(name, h)

    def ps(self, name, shape, dtype=F32):
        h = self.phase_es.enter_context(self.nc.psum_tensor("%s_p%d" % (name, self.n_phase), list(shape), dtype))
        return T(name, h, psum=True)

    def _track(self, op, reads, writes):
        deps = []
        for t in reads:
            if t.writer is not None:
                deps.append(t.writer)
            if t.psum:
                deps.extend(r for r in t.readers if r.eng != op.eng)
        for t in writes:
            if t.writer is not None:
                deps.append(t.writer)
            deps.extend(t.readers)
        seen = set()
        for d in deps:
            if d is op or id(d) in seen:
                continue
            seen.add(id(d))
            if d.eng == "pe" and op.eng == "pe" and not d.is_dma and not op.is_dma:
                continue
            op.deps.append(d)
            d.signal = True
        for t in reads:
            t.readers.append(op)
        for t in writes:
            t.writer = op
            t.readers = []

    def op(self, eng, fn, reads=(), writes=()):
        o = Op(eng, fn)
        self._track(o, reads, writes)
        self.ops.append(o)
        return o

    def dma(self, out, in_, reads=(), writes=(), queue="sp"):
        o = Op(queue, lambda e, out=out, in_=in_: e.dma_start(out=out, in_=in_), is_dma=True)
        o.signal = True
        self._track(o, reads, writes)
        self.ops.append(o)
        return o

    def end_phase(self):
        nc = self.nc
        ops = self.ops
        cnt = {e: 0 for e in ENGS}
        ndma = 0
        last_on_sem = {}
        all_dma = []
        for o in ops:
            if o.is_dma:
                s = ndma % N_DMA_SEMS
                prev = last_on_sem.get(s)
                if prev is not None:
                    o.deps.append(prev)
                o.sem = self.dma_sems[s]
                o.val = 16 * (ndma // N_DMA_SEMS + 1)
                last_on_sem[s] = o
                ndma += 1
                all_dma.append(o)
            elif o.signal:
                cnt[o.eng] += 1
                o.sem = self.sems[o.eng]
                o.val = cnt[o.eng]
        per_eng = {e: [o for o in ops if o.eng == e] for e in ENGS}
        final_dma = list(last_on_sem.values())
        final_cnt = dict(cnt)
        sems = self.sems
        dma_sems = self.dma_sems

        def emit(e, eng):
            waited = {}
            for o in per_eng[e]:
                for d in o.deps:
                    key = id(d.sem)
                    if waited.get(key, 0) >= d.val:
                        continue
                    eng.wait_ge(d.sem, d.val)
                    waited[key] = d.val
                ins = o.fn(eng)
                if o.is_dma:
                    ins.then_inc(o.sem, 16)
                elif o.signal:
                    ins.then_inc(o.sem, 1)
            if e == "sp":
                for d in final_dma:
                    if waited.get(id(d.sem), 0) < d.val:
                        eng.wait_ge(d.sem, d.val)
                for e2 in ENGS:
                    if e2 != "sp" and final_cnt[e2] > 0:
                        eng.wait_ge(sems[e2], final_cnt[e2])

        with nc.Block() as block:
            @block.tensor
            def _(eng):
                emit("pe", eng)

            @block.scalar
            def _(eng):
                emit("act", eng)

            @block.vector
            def _(eng):
                emit("dve", eng)

            @block.gpsimd
            def _(eng):
                emit("pool", eng)

            @block.sync
            def _(eng):
                emit("sp", eng)

        with nc.Block() as block:
            @block.tensor
            def _(eng):
                eng.sem_clear(sems["pe"])

            @block.scalar
            def _(eng):
                eng.sem_clear(sems["act"])

            @block.vector
            def _(eng):
                eng.sem_clear(sems["dve"])

            @block.gpsimd
            def _(eng):
                eng.sem_clear(sems["pool"])

            @block.sync
            def _(eng):
                eng.sem_clear(sems["sp"])
                for s in dma_sems:
                    eng.sem_clear(s)

        self.phase_es.close()
        self.phase_es = None
        self.ops = []
        self.n_phase += 1


class Ctx:
    pass


def load_featmajor_vec(p, cx, vec_ap, n, name, ps_tile):
    st = p.sb(name + "_st", [n, 128], F32)
    out = p.sb(name, [128, n], F32)
    p.dma(st[:, :], vec_ap.rearrange("(j q) -> j q", q=128), writes=[st])
    p.op("pe", lambda e: e.transpose(ps_tile[:, 0:n], st[:, :], cx.ident_f[0:n, 0:n]),
         reads=[st, cx.ident_f_t], writes=[ps_tile])
    p.op("dve", lambda e: e.tensor_copy(out[:, :], ps_tile[:, 0:n]), reads=[ps_tile], writes=[out])
    return out


def load_consts(p, cx):
    cx.ident_f_t = p.sb("ident_f", [128, 128], F32)
    cx.ident_b_t = p.sb("ident_b", [128, 128], BF16)
    cx.ident_f = cx.ident_f_t.h
    cx.ident_b = cx.ident_b_t.h
    p.dma(cx.ident_f_t[:, :], cx.d_ident_f[:, :], writes=[cx.ident_f_t])
    p.dma(cx.ident_b_t[:, :], cx.d_ident_b[:, :], writes=[cx.ident_b_t])
    mh = p.sb("mhalf", [128, 1], F32)
    p.op("pool", lambda e: e.memset(mh[:, :], -0.5), writes=[mh])
    NORM_CONST["mhalf"] = mh


def load_weight_cols(p, W_ap, K, c0, ncols, dst, d0, stage_tiles, gT=None, col_chunk=256, engs=("dve", "act"), ctr=None):
    kcs = K // 128
    ctr = ctr if ctr is not None else [0]
    for kc in range(kcs):
        for cc in range(0, ncols, col_chunk):
            cw = min(col_chunk, ncols - cc)
            i = ctr[0]
            ctr[0] += 1
            st = stage_tiles[i % len(stage_tiles)]
            eng = engs[i % len(engs)]
            p.dma(st[:, 0:cw], W_ap[kc * 128:(kc + 1) * 128, c0 + cc:c0 + cc + cw], writes=[st])
            o = dst[:, kc, d0 + cc:d0 + cc + cw]
            if eng == "act":
                if gT is not None:
                    p.op("act", lambda e, st=st, kc=kc, cw=cw, o=o: e.activation(o, st[:, 0:cw], AF.Copy, scale=gT[:, kc:kc + 1]),
                         reads=[st, gT], writes=[dst])
                else:
                    p.op("act", lambda e, st=st, cw=cw, o=o: e.activation(o, st[:, 0:cw], AF.Copy), reads=[st], writes=[dst])
            elif gT is not None:
                p.op(eng, lambda e, st=st, kc=kc, cw=cw, o=o: e.tensor_scalar(o, st[:, 0:cw], gT[:, kc:kc + 1], 1.0, ALU.mult, ALU.mult),
                     reads=[st, gT], writes=[dst])
            else:
                p.op(eng, lambda e, st=st, cw=cw, o=o: e.tensor_copy(o, st[:, 0:cw]), reads=[st], writes=[dst])


def load_weight_bf16(p, W_ap, K, N, dst, stage_tiles, gT=None, col_chunk=2048, engs=("dve", "act")):
    kcs = K // 128
    i = 0
    for kc in range(kcs):
        for c0 in range(0, N, col_chunk):
            cw = min(col_chunk, N - c0)
            st = stage_tiles[i % len(stage_tiles)]
            eng = engs[i % len(engs)]
            i += 1
            p.dma(st[:, 0:cw], W_ap[kc * 128:(kc + 1) * 128, c0:c0 + cw], writes=[st])
            if eng == "act":
                if gT is not None:
                    p.op("act", lambda e, st=st, kc=kc, c0=c0, cw=cw: e.activation(
                        dst[:, kc, c0:c0 + cw], st[:, 0:cw], AF.Copy, scale=gT[:, kc:kc + 1]),
                        reads=[st, gT], writes=[dst])
                else:
                    p.op("act", lambda e, st=st, kc=kc, c0=c0, cw=cw: e.activation(
                        dst[:, kc, c0:c0 + cw], st[:, 0:cw], AF.Copy), reads=[st], writes=[dst])
            elif gT is not None:
                p.op(eng, lambda e, st=st, kc=kc, c0=c0, cw=cw: e.tensor_scalar(
                    dst[:, kc, c0:c0 + cw], st[:, 0:cw], gT[:, kc:kc + 1], 1.0, ALU.mult, ALU.mult),
                    reads=[st, gT], writes=[dst])
            else:
                p.op(eng, lambda e, st=st, kc=kc, c0=c0, cw=cw: e.tensor_copy(dst[:, kc, c0:c0 + cw], st[:, 0:cw]),
                     reads=[st], writes=[dst])


def norm_tile(p, hb, xn, ss, rstd, junk, eps=RMS_EPS, d=D, cx=None):
    p.op("act", lambda e: e.activation(junk[:, :], hb[:, :], AF.Square, accum_out=ss[:, 0:1]),
         reads=[hb], writes=[junk, ss])
    p.op("dve", lambda e: e.tensor_scalar(rstd[:, 0:1], ss[:, 0:1], 1.0 / d, eps, ALU.mult, ALU.add),
         reads=[ss], writes=[rstd])
    p.op("pool", lambda e: e.tensor_tensor(rstd[:, 0:1], rstd[:, 0:1], NORM_CONST["mhalf"][:, 0:1], ALU.pow),
         reads=[rstd, NORM_CONST["mhalf"]], writes=[rstd])
    p.op("act", lambda e: e.activation(xn[:, :], hb[:, :], AF.Copy, scale=rstd[:, 0:1]),
         reads=[hb, rstd], writes=[xn])


NORM_CONST = {}


def transpose_to(p, cx, src, ncol_blocks, ps_tr, dst, dst_col0, evac_eng="dve"):
    for j0 in range(0, ncol_blocks, 8):
        nb = min(8, ncol_blocks - j0)
        for j in range(nb):
            p.op("pe", lambda e, j=j, j0=j0: e.transpose(ps_tr[:, j * 128:(j + 1) * 128],
                                                       src[:, (j0 + j) * 128:(j0 + j + 1) * 128], cx.ident_b[:, :]),
                 reads=[src, cx.ident_b_t], writes=[ps_tr])
        if evac_eng == "act":
            p.op("act", lambda e, j0=j0, nb=nb: e.activation(
                dst[:, j0:j0 + nb, dst_col0:dst_col0 + 128],
                ps_tr[:, 0:nb * 128].rearrange("p (j t) -> p j t", t=128), AF.Copy),
                reads=[ps_tr], writes=[dst])
        else:
            p.op(evac_eng, lambda e, j0=j0, nb=nb: e.tensor_copy(
                dst[:, j0:j0 + nb, dst_col0:dst_col0 + 128],
                ps_tr[:, 0:nb * 128].rearrange("p (j t) -> p j t", t=128)),
                reads=[ps_tr], writes=[dst])


def ffn_phase(p, cx, layer, h_in, h_out):
    p.begin_phase()
    load_consts(p, cx)
    psb = [p.ps("psb%d" % i, [128, 512], F32) for i in range(6)]
    ps_tr = [p.ps("pstr%d" % i, [128, 1024], BF16) for i in range(2)]
    Wig = [p.sb("Wi%d" % g, [128, 8, 512], BF16) for g in range(NFC // 2)]
    Wo = p.sb("Wo", [128, NFC, D], BF16)
    stage = [p.sb("wst%d" % i, [128, 352], F32) for i in range(4)]
    gT = load_featmajor_vec(p, cx, cx.norm_ffn_g[layer, :], 8, "gT", psb[0])
    cw = [load_featmajor_vec(p, cx, cx.ffn_conv_w[layer, k, :], NFC, "cw%d" % k, psb[0]) for k in range(3)]
    cb = load_featmajor_vec(p, cx, cx.ffn_conv_b[layer, :], NFC, "cb", psb[0])
    ctr = [0]

    def load_group(g):
        load_weight_cols(p, cx.ffn_w_in[layer], D, g * 256, 256, Wig[g], 0, stage, gT=gT, col_chunk=256, ctr=ctr)
        load_weight_cols(p, cx.ffn_w_in[layer], D, D_FF + g * 256, 256, Wig[g], 256, stage, gT=gT, col_chunk=256, ctr=ctr)

    def load_wo_rows(j):
        for c0 in range(0, D, 256):
            i = ctr[0]
            ctr[0] += 1
            st = stage[i % len(stage)]
            p.dma(st[:, 0:256], cx.ffn_w_out[layer][j * 128:(j + 1) * 128, c0:c0 + 256], writes=[st])
            if i % 2 == 0:
                p.op("dve", lambda e, st=st, c0=c0: e.tensor_copy(Wo[:, j, c0:c0 + 256], st[:, 0:256]), reads=[st], writes=[Wo])
            else:
                p.op("act", lambda e, st=st, c0=c0: e.activation(Wo[:, j, c0:c0 + 256], st[:, 0:256], AF.Copy),
                     reads=[st], writes=[Wo])

    halo = p.sb("halo", [128, NFC, 2], F32)
    p.op("pool", lambda e: e.memset(halo[:, :, :], 0.0), writes=[halo])
    hn = [p.sb("hn%d" % i, [128, D], F32) for i in range(2)]
    rb = [p.sb("rb%d" % i, [128, D], F32) for i in range(2)]
    xn = [p.sb("xn%d" % i, [128, D], BF16) for i in range(2)]
    ss = [p.sb("ss%d" % i, [128, 1], F32) for i in range(2)]
    rstd = [p.sb("rstd%d" % i, [128, 1], F32) for i in range(2)]
    XnT = [p.sb("XnT%d" % i, [128, 8, 512], BF16) for i in range(2)]
    A = p.sb("AT", [128, NFC, 512], BF16)
    G = [p.sb("G%d" % i, [128, 514], F32) for i in range(2)]
    c1 = [p.sb("c1_%d" % i, [128, 512], F32) for i in range(2)]

    NTT = S // 512

    def prep_n(tt, s):
        i = tt * 4 + s
        t = hn[i % 2]
        r0 = i * 128
        p.dma(t[:, :], h_in[r0:r0 + 128, :], writes=[t])
        k = i % 2
        norm_tile(p, t, xn[k], ss[k], rstd[k], xn[k])

    def prep_t(tt, s):
        i = tt * 4 + s
        k = i % 2
        transpose_to(p, cx, xn[k], 8, ps_tr[k], XnT[tt % 2], s * 128, evac_eng="dve")

    def prep(tt):
        for s in range(4):
            prep_n(tt, s)
            prep_t(tt, s)

    prep(0)
    load_group(0)
    for tt in range(NTT):
        X = XnT[tt % 2]
        for fc in range(NFC):
            if tt == 0:
                if fc % 2 == 0 and fc // 2 + 1 < NFC // 2:
                    load_group(fc // 2 + 1)
                load_wo_rows(fc)
            if tt + 1 < NTT and fc in (4, 6, 8, 10, 12):
                sidx = (fc - 4) // 2
                if sidx < 4:
                    prep_n(tt + 1, sidx)
                if sidx >= 1:
                    prep_t(tt + 1, sidx - 1)
            b = fc % 2
            gps, ups = psb[b], psb[2 + b]
            for kc in range(8):
                p.op("pe", lambda e, kc=kc, fc=fc, gps=gps, X=X: e.matmul(
                    gps[:, :], Wig[fc // 2][:, kc, (fc % 2) * 128:(fc % 2 + 1) * 128], X[:, kc, :],
                    start=(kc == 0), stop=(kc == 7)),
                    reads=[Wig[fc // 2], X], writes=[gps])
            for kc in range(8):
                p.op("pe", lambda e, kc=kc, fc=fc, ups=ups, X=X: e.matmul(
                    ups[:, :], Wig[fc // 2][:, kc, 256 + (fc % 2) * 128:256 + (fc % 2 + 1) * 128], X[:, kc, :],
                    start=(kc == 0), stop=(kc == 7)),
                    reads=[Wig[fc // 2], X], writes=[ups])
            g = G[b]
            p.op("pool", lambda e, g=g, fc=fc: e.tensor_copy(g[:, 0:2], halo[:, fc, :]), reads=[halo], writes=[g])
            p.op("act", lambda e, g=g, gps=gps: e.activation(g[:, 2:514], gps[:, :], AF.Copy), reads=[gps], writes=[g])
            p.op("pool", lambda e, g=g, fc=fc: e.tensor_copy(halo[:, fc, :], g[:, 512:514]), reads=[g], writes=[halo])
            p.op("act", lambda e, g=g, fc=fc, b=b: e.activation(
                c1[b][:, :], g[:, 0:512], AF.Identity, bias=cb[:, fc:fc + 1], scale=cw[0][:, fc:fc + 1]),
                reads=[g, cb, cw[0]], writes=[c1[b]])
            p.op("dve", lambda e, g=g, fc=fc, b=b: e.scalar_tensor_tensor(
                c1[b][:, :], g[:, 1:513], cw[1][:, fc:fc + 1], c1[b][:, :], ALU.mult, ALU.add),
                reads=[g, cw[1], c1[b]], writes=[c1[b]])
            p.op("dve", lambda e, g=g, fc=fc, b=b: e.scalar_tensor_tensor(
                c1[b][:, :], g[:, 2:514], cw[2][:, fc:fc + 1], c1[b][:, :], ALU.mult, ALU.add),
                reads=[g, cw[2], c1[b]], writes=[c1[b]])
            p.op("act", lambda e, b=b: e.activation(c1[b][:, :], c1[b][:, :], AF.Silu), reads=[c1[b]], writes=[c1[b]])
            p.op("dve", lambda e, b=b, fc=fc, ups=ups: e.tensor_tensor(A[:, fc, :], c1[b][:, :], ups[:, :], ALU.mult),
                 reads=[c1[b], ups], writes=[A])
        for s in range(4):
            i = tt * 4 + s
            r0 = i * 128
            hbt = rb[i % 2]
            if s == 0:
                p.dma(hbt[:, :], h_in[r0:r0 + 128, :], writes=[hbt])
            if s + 1 < 4:
                p.dma(rb[(i + 1) % 2][:, :], h_in[r0 + 128:r0 + 256, :], writes=[rb[(i + 1) % 2]])
            for half in range(2):
                ops_ = psb[4 + half]
                for fc in range(NFC):
                    p.op("pe", lambda e, fc=fc, s=s, half=half, ops_=ops_: e.matmul(
                        ops_[:, :], A[:, fc, s * 128:(s + 1) * 128], Wo[:, fc, half * 512:(half + 1) * 512],
                        start=(fc == 0), stop=(fc == NFC - 1)),
                        reads=[A, Wo], writes=[ops_])
                p.op("dve", lambda e, half=half, ops_=ops_, hbt=hbt: e.tensor_tensor(
                    hbt[:, half * 512:(half + 1) * 512], hbt[:, half * 512:(half + 1) * 512], ops_[:, :], ALU.add),
                    reads=[hbt, ops_], writes=[hbt])
            p.dma(h_out[r0:r0 + 128, :], hbt[:, :], reads=[hbt])
    p.end_phase()


def rope_bank(p, ps_bank, dst, dcol0, c2d, s2d, cs_tiles, nsub, hd, half, tmp):
    src = ps_bank.h[:, :].rearrange("p (n d) -> p n d", d=hd)
    dv = dst.h[:, dcol0:dcol0 + 512].rearrange("p (n d) -> p n d", d=hd)
    x1, x2 = src[:, :, 0:half], src[:, :, half:2 * half]
    c = c2d.rearrange("p (n d) -> p n d", d=half)
    s_ = s2d.rearrange("p (n d) -> p n d", d=half)
    w = nsub * half
    tv = [tt.h[:, 0:w].rearrange("p (n d) -> p n d", d=half) for tt in tmp]
    cs = list(cs_tiles)
    p.op("dve", lambda e: e.tensor_tensor(tv[0], x1, c, ALU.mult), reads=[ps_bank] + cs, writes=[tmp[0]])
    p.op("dve", lambda e: e.tensor_tensor(tv[1], x2, s_, ALU.mult), reads=[ps_bank] + cs, writes=[tmp[1]])
    p.op("dve", lambda e: e.tensor_tensor(dv[:, :, 0:half], tv[0], tv[1], ALU.subtract),
         reads=[tmp[0], tmp[1]], writes=[dst])
    p.op("dve", lambda e: e.tensor_tensor(tv[2], x2, c, ALU.mult), reads=[ps_bank] + cs, writes=[tmp[2]])
    p.op("dve", lambda e: e.tensor_tensor(tv[3], x1, s_, ALU.mult), reads=[ps_bank] + cs, writes=[tmp[3]])
    p.op("dve", lambda e: e.tensor_tensor(dv[:, :, half:2 * half], tv[2], tv[3], ALU.add),
         reads=[tmp[2], tmp[3]], writes=[dst])


def proj_phase(p, cx, layer, W_ap, N, h_in, setup, epi, tile_done):
    p.begin_phase()
    load_consts(p, cx)
    banks = [p.ps("pb%d" % i, [128, 512], F32) for i in range(6)]
    ps_tr = [p.ps("pstr%d" % i, [128, 1024], BF16) for i in range(2)]
    Wp = [p.sb("W%d" % i, [128, 8, 1536], BF16) for i in range(N // 1536)]
    stage = [p.sb("wst%d" % i, [128, 768], F32) for i in range(3)]
    gT = load_featmajor_vec(p, cx, cx.norm_mix_g[layer, :], 8, "gT", banks[0])
    ctr = [0]

    def load_pass(i):
        load_weight_cols(p, W_ap, D, i * 1536, 1536, Wp[i], 0, stage, gT=gT, col_chunk=768, ctr=ctr)
    st = setup(p, ps_tr)
    hn = [p.sb("hn%d" % i, [128, D], F32) for i in range(2)]
    xn = [p.sb("xn%d" % i, [128, D], BF16) for i in range(2)]
    ss = [p.sb("ss%d" % i, [128, 1], F32) for i in range(2)]
    rstd = [p.sb("rstd%d" % i, [128, 1], F32) for i in range(2)]
    XT = [p.sb("XT%d" % i, [128, 8, 128], BF16) for i in range(2)]
    npass = N // 1536
    ntl = NT if DBG_TILES is None else DBG_TILES

    def prep(t):
        k = t % 2
        p.dma(hn[k][:, :], h_in[t * 128:(t + 1) * 128, :], writes=[hn[k]])
        norm_tile(p, hn[k], xn[k], ss[k], rstd[k], xn[k])
        transpose_to(p, cx, xn[k], 8, ps_tr[0], XT[k], 0, evac_eng="act")

    prep(0)
    load_pass(0)
    for t in range(ntl):
        k = t % 2
        for ps_ in range(npass):
            if t == 0 and ps_ + 1 < npass:
                load_pass(ps_ + 1)
            bs = banks[(ps_ % 2) * 3:(ps_ % 2) * 3 + 3]
            for kc in range(8):
                for j in range(3):
                    c0 = j * 512
                    Wt = Wp[ps_]
                    p.op("pe", lambda e, kc=kc, j=j, c0=c0, bs=bs, k=k, Wt=Wt: e.matmul(
                        bs[j][:, :], XT[k][:, kc, :], Wt[:, kc, c0:c0 + 512], start=(kc == 0), stop=(kc == 7)),
                        reads=[XT[k], Wt], writes=[bs[j]])
            if ps_ == 0 and t + 1 < ntl:
                prep(t + 1)
            for j in range(3):
                epi(p, st, t, ps_ * 3 + j, bs[j])
        tile_done(p, st, t, ps_tr[1])
    p.end_phase()


def load_rope(p, cx, cos_ap, sin_ap, half):
    cosT = p.sb("cosT", [128, NT, half], F32)
    sinT = p.sb("sinT", [128, NT, half], F32)
    p.dma(cosT[:, :, :], cos_ap, writes=[cosT])
    p.dma(sinT[:, :, :], sin_ap, writes=[sinT])
    return cosT, sinT


def qkv_proj_phase(p, cx, layer, W_ap, h_in, cos_ap, sin_ap, nsub, hd, half, QT_d, KT_d, V_d, KM_d=None):
    def setup(p, ps_tr):
        st = Ctx()
        st.cosT, st.sinT = load_rope(p, cx, cos_ap, sin_ap, nsub * half)
        st.qk = [p.sb("qk%d" % i, [128, 2048], BF16) for i in range(2)]
        st.vb = [p.sb("vb%d" % i, [128, 1024], BF16) for i in range(2)]
        st.tmp = [p.sb("rt%d" % i, [128, 64], F32) for i in range(4)]
        st.QKT = [p.sb("QKT%d" % i, [128, 16, 512], BF16) for i in range(2)]
        if KM_d is not None:
            st.kms = p.sb("kms", [128, 8, 16], F32)
        return st

    def epi(p, st, t, gb, bank):
        k = t % 2
        if gb < 4:
            p.op("act", lambda e: e.activation(st.qk[k][:, gb * 512:(gb + 1) * 512], bank[:, :], AF.Copy),
                 reads=[bank], writes=[st.qk[k]])
            rope_bank(p, bank, st.qk[k], gb * 512, st.cosT[:, t, :], st.sinT[:, t, :], [st.cosT, st.sinT], nsub, hd, half, st.tmp)
        else:
            p.op("act", lambda e: e.activation(st.vb[k][:, (gb - 4) * 512:(gb - 3) * 512], bank[:, :], AF.Copy),
                 reads=[bank], writes=[st.vb[k]])

    def tile_done(p, st, t, ps_tr):
        k = t % 2
        p.dma(V_d[t * 128:(t + 1) * 128, :], st.vb[k][:, :], reads=[st.vb[k]])
        g = (t // 4) % 2
        transpose_to(p, cx, st.qk[k], 16, ps_tr, st.QKT[g], (t % 4) * 128, evac_eng="act")
        if t % 4 == 3:
            t0 = (t // 4) * 512
            for hh in range(8):
                p.dma(QT_d[hh, :, t0:t0 + 512], st.QKT[g][:, hh, :], reads=[st.QKT[g]])
                p.dma(KT_d[hh, :, t0:t0 + 512], st.QKT[g][:, 8 + hh, :], reads=[st.QKT[g]])
            if KM_d is not None:
                n0 = (t // 4) * 2
                p.op("dve", lambda e: e.tensor_reduce(
                    st.kms[:, :, n0:n0 + 2], st.QKT[g][:, 8:16, :].rearrange("p h (n k) -> p h n k", k=256), AX.X, ALU.add),
                    reads=[st.QKT[g]], writes=[st.kms])
                if t == NT - 1:
                    p.dma(KM_d[:, :, :], st.kms[:, :, :], reads=[st.kms])

    proj_phase(p, cx, layer, W_ap, 3072, h_in, setup, epi, tile_done)


def out_proj_residual(p, cx, OT, nk, Wo, h_in, h_out, tok0, rb, ops2, cnt):
    nb = len(rb)
    base = cnt[0]
    cnt[0] += 4

    def load(s):
        r0 = tok0 + s * 128
        hbt = rb[(base + s) % nb]
        p.dma(hbt[:, :], h_in[r0:r0 + 128, :], writes=[hbt])

    load(0)
    for s in range(4):
        if s + 1 < 4:
            load(s + 1)
        r0 = tok0 + s * 128
        hbt = rb[(base + s) % nb]
        for half in range(2):
            ops_ = ops2[half]
            for kc in range(nk):
                p.op("pe", lambda e, kc=kc, s=s, half=half, ops_=ops_: e.matmul(
                    ops_[:, :], OT[:, kc, s * 128:(s + 1) * 128], Wo[:, kc, half * 512:(half + 1) * 512],
                    start=(kc == 0), stop=(kc == nk - 1)),
                    reads=[OT, Wo], writes=[ops_])
            p.op("dve", lambda e, half=half, ops_=ops_, hbt=hbt: e.tensor_tensor(
                hbt[:, half * 512:(half + 1) * 512], hbt[:, half * 512:(half + 1) * 512], ops_[:, :], ALU.add),
                reads=[hbt, ops_], writes=[hbt])
        p.dma(h_out[r0:r0 + 128, :], hbt[:, :], reads=[hbt])


def run_skewed(items, la, hooks=None):
    n = len(items)
    for i in range(min(la, n)):
        items[i][0]()
    for j in range(n):
        items[j][1]()
        if j + la < n:
            items[j + la][0]()
        if hooks and j in hooks:
            hk = hooks[j]
            for f in (hk if isinstance(hk, list) else [hk]):
                f()


def moba_attn_phase(p, cx, h_in, h_out, QT_d, KT_d, V_d, KM_d):
    p.begin_phase()
    load_consts(p, cx)
    sT = [p.ps("sT%d" % i, [128, 512], F32) for i in range(2)]
    oT = [p.ps("oT%d" % i, [128, 512], F32) for i in range(2)]
    sm = [p.ps("sm%d" % i, [128, 512], F32) for i in range(2)]
    misc = p.ps("misc", [128, 512], F32)
    ops1 = p.ps("ops1", [128, 512], F32)
    KTh = [p.sb("KT%d" % h, [128, S], BF16) for h in range(8)]
    Vg = [p.sb("V%d" % g, [128, 4, D], BF16) for g in range(8)]
    Wo = p.sb("Wo", [128, 8, D], BF16)
    stage = [p.sb("wst%d" % i, [128, 256], F32) for i in range(2)]
    for g in range(8):
        p.dma(KTh[g][:, :], KT_d[g, :, :], writes=[KTh[g]])
        if g == 0:
            for t4 in range(0, 4):
                p.dma(Vg[0][:, t4 % 4, :], V_d[t4 * 128:(t4 + 1) * 128, :], writes=[Vg[0]])
    for g in range(1, 8):
        for t4 in range(4 * g, 4 * g + 4):
            p.dma(Vg[g][:, t4 % 4, :], V_d[t4 * 128:(t4 + 1) * 128, :], writes=[Vg[g]])
    load_weight_bf16(p, cx.a_w_out, D, D, Wo, stage, col_chunk=256)
    ones_b = p.sb("ones_b", [128, 128], BF16)
    p.op("pool", lambda e: e.memset(ones_b[:, :], 1.0), writes=[ones_b])
    CM = p.sb("CM", [128, 4, 512], BF16)
    p.dma(CM[:, :, :], cx.d_cmask[:, :, :], writes=[CM])
    En = p.sb("En", [128, 16, 128], BF16)
    p.op("pool", lambda e: e.memset(En[:, :, :], 0.0), writes=[En])
    p.dma(En[0:16, :, :], cx.d_en[:, :, :], writes=[En])
    PB2 = p.sb("PB2", [128, 16, 16], F32)
    p.dma(PB2[:, :, :], cx.d_pb2[:, :, :], writes=[PB2])
    kmf = p.sb("kmf", [128, 8, 16], F32)
    kmT = p.sb("kmT", [128, 8, 16], BF16)
    p.dma(kmf[:, :, :], KM_d[:, :, :], writes=[kmf])
    p.op("dve", lambda e: e.tensor_scalar(kmT[:, :, :], kmf[:, :, :], 1.0 / 256, None, ALU.mult), reads=[kmf], writes=[kmT])

    QT = [p.sb("QT%d" % i, [128, 8, 512], BF16) for i in range(2)]
    gm = [p.sb("gm%d" % i, [128, 16], F32) for i in range(2)]
    top8 = p.sb("top8", [128, 8, 8], F32)
    thr = p.sb("thr", [128, 8], F32)
    bft = p.sb("bft", [128, 8, 16], F32)
    btok = [p.sb("btok%d" % i, [128, 8, 16], BF16) for i in range(4)]
    bT = p.sb("biasT", [128, 8, 512], BF16)
    p.op("pool", lambda e: e.memset(bT[:, :, :], 0.0), writes=[bT])
    PT = [p.sb("PT%d" % i, [128, 512], BF16) for i in range(3)]
    rs = p.sb("rs", [128, 512], F32)
    OT = p.sb("OT", [128, 8, 512], BF16)
    rb = [p.sb("rb%d" % i, [128, D], F32) for i in range(2)]
    cnt = [0]
    scale = 128.0 ** -0.5
    NQ = S // 512 if DBG_TILES is None else DBG_TILES // 4
    pic = [0]
    misc_b = misc.h[:, :].bitcast(BF16)

    def load_q(Q):
        for hh in range(8):
            p.dma(QT[Q % 2][:, hh, :], QT_d[hh, :, Q * 512:(Q + 1) * 512], writes=[QT[Q % 2]])

    def gate1(Q):
        q = QT[Q % 2]
        for s in range(4):
            for h in range(8):
                p.op("pe", lambda e, s=s, h=h: e.matmul(
                    misc[:, (s * 8 + h) * 16:(s * 8 + h + 1) * 16], q[:, h, s * 128:(s + 1) * 128], kmT[:, h, :],
                    start=True, stop=True), reads=[q, kmT], writes=[misc])
        for s in range(4):
            qb = 2 * Q + s // 2
            bt = btok[s]
            for h in range(8):
                g_ = gm[h % 2]
                p.op("dve", lambda e, s=s, h=h, g_=g_, qb=qb: e.tensor_tensor(
                    g_[:, :], misc[:, (s * 8 + h) * 16:(s * 8 + h + 1) * 16], PB2[:, qb, :], ALU.add),
                    reads=[misc, PB2], writes=[g_])
                p.op("dve", lambda e, h=h, g_=g_: e.max(top8[:, h, :], g_[:, :]), reads=[g_], writes=[top8])
                p.op("dve", lambda e, h=h: e.tensor_scalar(thr[:, h:h + 1], top8[:, h, 3:4], -1e29, None, ALU.max),
                     reads=[top8], writes=[thr])
                p.op("dve", lambda e, h=h, g_=g_: e.tensor_scalar(
                    bft[:, h, :], g_[:, :], thr[:, h:h + 1], -NEG, ALU.is_ge, ALU.mult),
                    reads=[g_, thr], writes=[bft])
            p.op("dve", lambda e, bt=bt: e.tensor_scalar(bt[:, :, :], bft[:, :, :], NEG, None, ALU.add),
                 reads=[bft], writes=[bt])

    def gate2(Q):
        for s in range(4):
            bt = btok[s]
            for h in range(8):
                p.op("pe", lambda e, h=h, bt=bt: e.transpose(
                    misc_b[0:16, h * 128:(h + 1) * 128], bt[:, h, :], cx.ident_b[:, :]),
                    reads=[bt, cx.ident_b_t], writes=[misc])
            p.op("act", lambda e, s=s: e.activation(
                bT[0:16, :, s * 128:(s + 1) * 128],
                misc_b[0:16, 0:1024].rearrange("p (h t) -> p h t", t=128), AF.Copy),
                reads=[misc], writes=[bT])

    def attention(Q, mid_hook):
        q = QT[Q % 2]
        items = []
        nkt = 4 * Q + 4
        for h in range(8):
            o_ps, s_ps = oT[h % 2], sm[h % 2]
            for kt in range(nkt):
                def sa(h=h, kt=kt):
                    pi = pic[0]
                    pic[0] += 1
                    st_, pt = sT[pi % 2], PT[pi % 3]
                    diag = kt >= 4 * Q
                    p.op("pe", lambda e: e.matmul(
                        st_[:, :], KTh[h][:, kt * 128:(kt + 1) * 128], q[:, h, :], start=True, stop=False),
                        reads=[KTh[h], q], writes=[st_])
                    p.op("pe", lambda e: e.matmul(
                        st_[:, :], En[:, kt // 2, :], bT[:, h, :], start=False, stop=(not diag)),
                        reads=[En, bT], writes=[st_])
                    if diag:
                        p.op("pe", lambda e: e.matmul(
                            st_[:, :], cx.ident_b[:, :], CM[:, kt - 4 * Q, :], start=False, stop=True),
                            reads=[CM, cx.ident_b_t], writes=[st_])
                    p.op("act", lambda e: e.activation(pt[:, :], st_[:, :], AF.Exp, scale=scale),
                         reads=[st_], writes=[pt])
                    return pt

                def sb_(h=h, kt=kt, o_ps=o_ps, s_ps=s_ps, holder=None):
                    pt = holder[0]
                    p.op("pe", lambda e: e.matmul(
                        o_ps[:, :], Vg[kt // 4][:, kt % 4, h * 128:(h + 1) * 128], pt[:, :],
                        start=(kt == 0), stop=(kt == nkt - 1)),
                        reads=[Vg[kt // 4], pt], writes=[o_ps])
                    p.op("pe", lambda e: e.matmul(
                        s_ps[:, :], ones_b[:, :], pt[:, :], start=(kt == 0), stop=(kt == nkt - 1)),
                        reads=[ones_b, pt], writes=[s_ps])
                    if kt == nkt - 1:
                        p.op("dve", lambda e: e.reciprocal(rs[:, :], s_ps[:, :]), reads=[s_ps], writes=[rs])
                        p.op("dve", lambda e: e.tensor_tensor(OT[:, h, :], o_ps[:, :], rs[:, :], ALU.mult),
                             reads=[o_ps, rs], writes=[OT])

                holder = [None]
                items.append(((lambda sa=sa, holder=holder: holder.__setitem__(0, sa())),
                              (lambda sb_=sb_, holder=holder: sb_(holder=holder))))
        hooks = {len(items) // 2: mid_hook} if mid_hook else None
        run_skewed(items, 2, hooks)

    load_q(0)
    gate1(0)
    gate2(0)
    for Q in range(NQ):
        nxt = None
        if Q + 1 < NQ:
            load_q(Q + 1)
            nxt = (lambda Q=Q: gate1(Q + 1))
        attention(Q, nxt)
        out_proj_residual(p, cx, OT, 8, Wo, h_in, h_out, Q * 512, rb, [ops1, ops1], cnt)
        if Q + 1 < NQ:
            gate2(Q + 1)
    p.end_phase()


def diff_attn_phase(p, cx, h_in, h_out, QT_d, KT_d, V_d):
    p.begin_phase()
    load_consts(p, cx)
    sT = [p.ps("sT%d" % i, [128, 512], F32) for i in range(2)]
    oT = [p.ps("oT%d" % i, [128, 512], F32) for i in range(2)]
    sm = [p.ps("sm%d" % i, [128, 512], F32) for i in range(2)]
    ops2 = [p.ps("ops%d" % i, [128, 512], F32) for i in range(2)]
    KTh = [p.sb("KT%d" % h, [128, S], BF16) for h in range(8)]
    Vg = [p.sb("V%d" % g, [128, 4, D], BF16) for g in range(8)]
    Wo = p.sb("Wo", [128, 8, D], BF16)
    stage = [p.sb("wst%d" % i, [128, 512], F32) for i in range(2)]
    for g in range(8):
        p.dma(KTh[g][:, :], KT_d[g, :, :], writes=[KTh[g]])
        for t4 in range(4 * g, 4 * g + 4):
            p.dma(Vg[g][:, t4 % 4, :], V_d[t4 * 128:(t4 + 1) * 128, :], writes=[Vg[g]])
    load_weight_bf16(p, cx.b_w_out, D, D, Wo, stage, col_chunk=512)
    ones_b = p.sb("ones_b", [128, 128], BF16)
    p.op("pool", lambda e: e.memset(ones_b[:, :], 1.0), writes=[ones_b])
    CM = p.sb("CM", [128, 4, 512], BF16)
    p.dma(CM[:, :, :], cx.d_cmask[:, :, :], writes=[CM])
    lam_init = 0.8 - 0.6 * math.exp(-0.3 * 1)
    lv = p.sb("lv", [1, 4, 64], F32)
    for i, nm in enumerate(("b_lam_q1", "b_lam_k1", "b_lam_q2", "b_lam_k2")):
        p.dma(lv[:, i, :], getattr(cx, nm).rearrange("(o d) -> o d", o=1), writes=[lv])
    pr = p.sb("pr", [1, 2, 64], F32)
    p.op("dve", lambda e: e.tensor_tensor(pr[:, 0, :], lv[:, 0, :], lv[:, 1, :], ALU.mult), reads=[lv], writes=[pr])
    p.op("dve", lambda e: e.tensor_tensor(pr[:, 1, :], lv[:, 2, :], lv[:, 3, :], ALU.mult), reads=[lv], writes=[pr])
    sv = p.sb("sv", [1, 2], F32)
    p.op("dve", lambda e: e.tensor_reduce(sv[:, :], pr[:, :, :], AX.X, ALU.add), reads=[pr], writes=[sv])
    p.op("act", lambda e: e.activation(sv[:, :], sv[:, :], AF.Exp), reads=[sv], writes=[sv])
    nl = p.sb("nl", [1, 2], F32)
    p.op("dve", lambda e: e.tensor_tensor(nl[:, 0:1], sv[:, 1:2], sv[:, 0:1], ALU.subtract), reads=[sv], writes=[nl])
    p.op("dve", lambda e: e.tensor_scalar(nl[:, 0:1], nl[:, 0:1], -lam_init, None, ALU.add), reads=[nl], writes=[nl])
    p.op("dve", lambda e: e.tensor_copy(nl[:, 1:2], nl[:, 0:1]), reads=[nl], writes=[nl])
    ones_f = p.sb("ones_f", [1, 128], F32)
    p.op("pool", lambda e: e.memset(ones_f[:, :], 1.0), writes=[ones_f])
    p.op("pe", lambda e: e.matmul(ops2[0][:, 0:2], ones_f[:, :], nl[:, :], start=True, stop=True),
         reads=[ones_f, nl], writes=[ops2[0]])
    nlam = p.sb("nlam", [128, 1], F32)
    p.op("dve", lambda e: e.tensor_copy(nlam[:, :], ops2[0][:, 0:1]), reads=[ops2[0]], writes=[nlam])
    gsc = load_featmajor_vec(p, cx, cx.b_subln_g, 1, "gsc", ops2[1])
    p.op("dve", lambda e: e.tensor_scalar(gsc[:, :], gsc[:, :], 1.0 - lam_init, None, ALU.mult), reads=[gsc], writes=[gsc])

    Qz = [p.sb("Qz%d" % i, [128, 8, 512], BF16) for i in range(2)]
    for i in range(2):
        p.op("pool", lambda e, i=i: e.memset(Qz[i][:, :, :], 0.0), writes=[Qz[i]])
    PT = [p.sb("PT%d" % i, [128, 512], BF16) for i in range(3)]
    rs = p.sb("rs", [128, 512], F32)
    rs2 = p.sb("rs2", [128, 512], F32)
    ots = [p.sb("ots%d" % i, [128, 512], F32) for i in range(2)]
    sms = [p.sb("sms%d" % i, [128, 512], F32) for i in range(2)]
    o1 = p.sb("o1", [128, 512], F32)
    osq = p.sb("osq", [128, 512], BF16)
    OT = p.sb("OT", [128, 8, 512], BF16)
    rb = [p.sb("rb%d" % i, [128, D], F32) for i in range(2)]
    cnt = [0]
    scale = 64.0 ** -0.5
    NQ = S // 512 if DBG_TILES is None else DBG_TILES // 4
    pic = [0]
    sq_ps = ops2[1]

    def load_q(Q):
        for hh in range(8):
            for i in range(2):
                p.dma(Qz[i][i * 64:(i + 1) * 64, hh, :], QT_d[hh, i * 64:(i + 1) * 64, Q * 512:(Q + 1) * 512],
                      writes=[Qz[i]])

    def epi1(h):
        p.op("dve", lambda e: e.reciprocal(rs[:, :], sms[0][:, :]), reads=[sms[0]], writes=[rs])
        p.op("dve", lambda e: e.tensor_tensor(o1[:, :], ots[0][:, :], rs[:, :], ALU.mult), reads=[ots[0], rs], writes=[o1])
        p.op("dve", lambda e: e.reciprocal(rs[:, :], sms[1][:, :]), reads=[sms[1]], writes=[rs])
        p.op("dve", lambda e: e.tensor_tensor(rs[:, :], ots[1][:, :], rs[:, :], ALU.mult), reads=[ots[1], rs], writes=[rs])
        p.op("dve", lambda e: e.scalar_tensor_tensor(o1[:, :], rs[:, :], nlam[:, 0:1], o1[:, :], ALU.mult, ALU.add),
             reads=[o1, rs, nlam], writes=[o1])
        p.op("dve", lambda e: e.tensor_tensor(osq[:, :], o1[:, :], o1[:, :], ALU.mult), reads=[o1], writes=[osq])

    def epi2(h):
        p.op("pe", lambda e: e.matmul(sq_ps[:, :], ones_b[:, :], osq[:, :], start=True, stop=True),
             reads=[ones_b, osq], writes=[sq_ps])
        p.op("dve", lambda e: e.tensor_scalar(rs2[:, :], sq_ps[:, :], 1.0 / 128, 1e-5, ALU.mult, ALU.add),
             reads=[sq_ps], writes=[rs2])
        p.op("act", lambda e: e.activation(rs2[:, :], rs2[:, :], AF.Ln), reads=[rs2], writes=[rs2])
        p.op("act", lambda e: e.activation(rs2[:, :], rs2[:, :], AF.Exp, scale=-0.5), reads=[rs2], writes=[rs2])
        p.op("dve", lambda e: e.scalar_tensor_tensor(OT[:, h, :], o1[:, :], gsc[:, 0:1], rs2[:, :], ALU.mult, ALU.mult),
             reads=[o1, gsc, rs2], writes=[OT])

    def attention(Q):
        nkt = 4 * Q + 4
        items = []
        hooks = {}
        for h in range(8):
            for kt in range(nkt):
                for i in range(2):
                    o_ps, s_ps = oT[i], sm[i]

                    def sa(h=h, i=i, kt=kt):
                        pi = pic[0]
                        pic[0] += 1
                        st_, pt = sT[pi % 2], PT[pi % 3]
                        diag = kt >= 4 * Q
                        p.op("pe", lambda e: e.matmul(
                            st_[:, :], KTh[h][:, kt * 128:(kt + 1) * 128], Qz[i][:, h, :],
                            start=True, stop=(not diag)), reads=[KTh[h], Qz[i]], writes=[st_])
                        if diag:
                            p.op("pe", lambda e: e.matmul(
                                st_[:, :], cx.ident_b[:, :], CM[:, kt - 4 * Q, :], start=False, stop=True),
                                reads=[CM, cx.ident_b_t], writes=[st_])
                        p.op("act", lambda e: e.activation(pt[:, :], st_[:, :], AF.Exp, scale=scale),
                             reads=[st_], writes=[pt])
                        return pt

                    def sb_(h=h, i=i, kt=kt, o_ps=o_ps, s_ps=s_ps, holder=None):
                        pt = holder[0]
                        p.op("pe", lambda e: e.matmul(
                            o_ps[:, :], Vg[kt // 4][:, kt % 4, h * 128:(h + 1) * 128], pt[:, :],
                            start=(kt == 0), stop=(kt == nkt - 1)),
                            reads=[Vg[kt // 4], pt], writes=[o_ps])
                        p.op("pe", lambda e: e.matmul(
                            s_ps[:, :], ones_b[:, :], pt[:, :], start=(kt == 0), stop=(kt == nkt - 1)),
                            reads=[ones_b, pt], writes=[s_ps])
                        if kt == nkt - 1:
                            p.op("dve", lambda e: e.tensor_copy(sms[i][:, :], s_ps[:, :]), reads=[s_ps], writes=[sms[i]])
                            p.op("dve", lambda e: e.tensor_copy(ots[i][:, :], o_ps[:, :]), reads=[o_ps], writes=[ots[i]])
                            if i == 1:
                                epi1(h)

                    holder = [None]
                    items.append(((lambda sa=sa, holder=holder: holder.__setitem__(0, sa())),
                                  (lambda sb_=sb_, holder=holder: sb_(holder=holder))))
            last = len(items) - 1
            hooks.setdefault(min(last + min(20, 2 * nkt - 2), 16 * nkt - 1), []).append(lambda h=h: epi2(h))
        run_skewed(items, 2, hooks)

    for Q in range(NQ):
        load_q(Q)
        attention(Q)
        out_proj_residual(p, cx, OT, 8, Wo, h_in, h_out, Q * 512, rb, [ops2[0], ops2[0]], cnt)
    p.end_phase()


def rglru_phase(p, cx, layer, h_in, h_out):
    p.begin_phase()
    load_consts(p, cx)
    psb = [p.ps("psb%d" % i, [128, 512], F32) for i in range(4)]
    psg = [p.ps("psg%d" % i, [128, 512], F32) for i in range(4)]
    Wi = p.sb("Wi", [128, 8, 2048], BF16)
    Wo = p.sb("Wo", [128, 8, D], BF16)
    Wa = p.sb("Wa", [128, 8, 128], BF16)
    Wx = p.sb("Wx", [128, 8, 128], BF16)
    stage = [p.sb("wst%d" % i, [128, 1024], F32) for i in range(2)]
    gT = load_featmajor_vec(p, cx, cx.norm_mix_g[layer, :], 8, "gT", psb[0])
    cw = [load_featmajor_vec(p, cx, cx.c_conv_w[k, :], 8, "cw%d" % k, psb[0]) for k in range(4)]
    cb = load_featmajor_vec(p, cx, cx.c_conv_b, 8, "cb", psb[0])
    ba = load_featmajor_vec(p, cx, cx.c_b_a, 8, "ba", psb[0])
    bx = load_featmajor_vec(p, cx, cx.c_b_x, 8, "bx", psb[0])
    lamT = load_featmajor_vec(p, cx, cx.c_lambda, 8, "lamT", psb[0])
    cf = p.sb("cf", [128, 8], F32)
    cf2 = p.sb("cf2", [128, 8], F32)
    p.op("act", lambda e: e.activation(cf[:, :], lamT[:, :], AF.Exp, scale=-1.0), reads=[lamT], writes=[cf])
    p.op("dve", lambda e: e.tensor_scalar(cf[:, :], cf[:, :], 1.0, None, ALU.add), reads=[cf], writes=[cf])
    p.op("act", lambda e: e.activation(cf[:, :], cf[:, :], AF.Ln), reads=[cf], writes=[cf])
    p.op("dve", lambda e: e.tensor_scalar(cf2[:, :], cf[:, :], -16.0, None, ALU.mult), reads=[cf], writes=[cf2])
    p.op("dve", lambda e: e.tensor_scalar(cf[:, :], cf[:, :], -8.0, None, ALU.mult), reads=[cf], writes=[cf])
    load_weight_bf16(p, cx.c_w_in, D, 2048, Wi, stage, gT=gT, col_chunk=1024)
    load_weight_bf16(p, cx.c_w_out, D, D, Wo, stage, col_chunk=1024)
    for g in range(8):
        st = stage[g % 2]
        p.dma(st[:, 0:128], cx.c_w_a[g, :, :], writes=[st])
        p.op("dve", lambda e, st=st, g=g: e.tensor_copy(Wa[:, g, :], st[:, 0:128]), reads=[st], writes=[Wa])
        p.dma(st[:, 128:256], cx.c_w_x[g, :, :], writes=[st])
        p.op("dve", lambda e, st=st, g=g: e.tensor_copy(Wx[:, g, :], st[:, 128:256]), reads=[st], writes=[Wx])

    halo = p.sb("halo", [128, 8, 3], F32)
    p.op("pool", lambda e: e.memset(halo[:, :, :], 0.0), writes=[halo])
    carry = p.sb("carry", [128, 8], F32)
    p.op("pool", lambda e: e.memset(carry[:, :], 0.0), writes=[carry])
    hn = [p.sb("hn%d" % i, [128, D], F32) for i in range(2)]
    rb = [p.sb("rb%d" % i, [128, D], F32) for i in range(2)]
    xn = [p.sb("xn%d" % i, [128, D], BF16) for i in range(2)]
    ss = [p.sb("ss%d" % i, [128, 1], F32) for i in range(2)]
    rstd = [p.sb("rstd%d" % i, [128, 1], F32) for i in range(2)]
    XnT = [p.sb("XnT%d" % i, [128, 8, 512], BF16) for i in range(2)]
    YT = p.sb("YT", [128, 8, 512], BF16)
    R = [p.sb("R%d" % i, [128, 515], F32) for i in range(4)]
    cv = [p.sb("cv%d" % i, [128, 512], F32) for i in range(4)]
    cvb = [p.sb("cvb%d" % i, [128, 512], BF16) for i in range(4)]
    rr = [p.sb("rr%d" % i, [128, 512], F32) for i in range(4)]
    ii = [p.sb("ii%d" % i, [128, 512], F32) for i in range(4)]
    aa = [p.sb("aa%d" % i, [128, 512], F32) for i in range(4)]
    mm = [p.sb("mm%d" % i, [128, 512], F32) for i in range(4)]
    hs = [p.sb("hs%d" % i, [128, 512], F32) for i in range(4)]
    gq = [p.sb("gq%d" % i, [128, 512], F32) for i in range(4)]
    gsb = [p.sb("gsb%d" % i, [128, 512], F32) for i in range(4)]
    cnt = [0]
    NTT = S // 512 if DBG_TILES is None else DBG_TILES // 4

    def prep(tt):
        X = XnT[tt % 2]
        trb = psb[2].h[:, :].bitcast(BF16)
        for s in range(4):
            i = tt * 4 + s
            t = hn[i % 2]
            r0 = i * 128
            p.dma(t[:, :], h_in[r0:r0 + 128, :], writes=[t])
            k = i % 2
            norm_tile(p, t, xn[k], ss[k], rstd[k], xn[k])
            for j in range(8):
                p.op("pe", lambda e, j=j, k=k: e.transpose(trb[:, j * 128:(j + 1) * 128],
                                                          xn[k][:, j * 128:(j + 1) * 128], cx.ident_b[:, :]),
                     reads=[xn[k], cx.ident_b_t], writes=[psb[2]])
            p.op("dve", lambda e, s=s, X=X: e.tensor_copy(
                X[:, :, s * 128:(s + 1) * 128], trb[:, 0:1024].rearrange("p (j t) -> p j t", t=128)),
                reads=[psb[2]], writes=[X])

    def chunk_gen(tt, c):
        X = XnT[tt % 2]
        b = c % 4
        gps, rps = psb[b % 2], psb[2 + b % 2]
        pa, px = psg[2 * (b % 2)], psg[2 * (b % 2) + 1]
        gs = gsb[b]
        for kc in range(8):
            p.op("pe", lambda e, kc=kc: e.matmul(
                gps[:, :], Wi[:, kc, c * 128:(c + 1) * 128], X[:, kc, :], start=(kc == 0), stop=(kc == 7)),
                reads=[Wi, X], writes=[gps])
        for kc in range(8):
            p.op("pe", lambda e, kc=kc: e.matmul(
                rps[:, :], Wi[:, kc, 1024 + c * 128:1024 + (c + 1) * 128], X[:, kc, :],
                start=(kc == 0), stop=(kc == 7)),
                reads=[Wi, X], writes=[rps])
        r_ = R[b]
        p.op("pool", lambda e: e.tensor_copy(r_[:, 0:3], halo[:, c, :]), reads=[halo], writes=[r_])
        p.op("act", lambda e: e.activation(r_[:, 3:515], rps[:, :], AF.Copy), reads=[rps], writes=[r_])
        p.op("act", lambda e: e.activation(gs[:, :], gps[:, :], AF.Copy), reads=[gps], writes=[gs])
        p.op("pool", lambda e: e.tensor_copy(halo[:, c, :], r_[:, 512:515]), reads=[r_], writes=[halo])
        yield
        p.op("act", lambda e: e.activation(
            cv[b][:, :], r_[:, 0:512], AF.Identity, bias=cb[:, c:c + 1], scale=cw[0][:, c:c + 1]),
            reads=[r_, cb, cw[0]], writes=[cv[b]])
        yield
        for k in range(1, 4):
            p.op("dve", lambda e, k=k: e.scalar_tensor_tensor(
                cv[b][:, :], r_[:, k:k + 512], cw[k][:, c:c + 1], cv[b][:, :], ALU.mult, ALU.add),
                reads=[r_, cw[k], cv[b]], writes=[cv[b]])
            yield
        p.op("act", lambda e: e.activation(cvb[b][:, :], cv[b][:, :], AF.Copy), reads=[cv[b]], writes=[cvb[b]])
        yield
        p.op("pe", lambda e: e.matmul(pa[:, :], Wa[:, c, :], cvb[b][:, :], start=True, stop=True),
             reads=[Wa, cvb[b]], writes=[pa])
        p.op("pe", lambda e: e.matmul(px[:, :], Wx[:, c, :], cvb[b][:, :], start=True, stop=True),
             reads=[Wx, cvb[b]], writes=[px])
        p.op("act", lambda e: e.activation(rr[b][:, :], pa[:, :], AF.Sigmoid, bias=ba[:, c:c + 1]),
             reads=[pa, ba], writes=[rr[b]])
        p.op("act", lambda e: e.activation(ii[b][:, :], px[:, :], AF.Sigmoid, bias=bx[:, c:c + 1]),
             reads=[px, bx], writes=[ii[b]])
        yield
        p.op("act", lambda e: e.activation(aa[b][:, :], rr[b][:, :], AF.Exp, scale=cf[:, c:c + 1]),
             reads=[rr[b], cf], writes=[aa[b]])
        p.op("act", lambda e: e.activation(mm[b][:, :], rr[b][:, :], AF.Exp, scale=cf2[:, c:c + 1]),
             reads=[rr[b], cf2], writes=[mm[b]])
        yield
        p.op("dve", lambda e: e.tensor_scalar(mm[b][:, :], mm[b][:, :], -1.0, 1.0, ALU.mult, ALU.add),
             reads=[mm[b]], writes=[mm[b]])
        p.op("dve", lambda e: e.tensor_tensor(ii[b][:, :], ii[b][:, :], cv[b][:, :], ALU.mult),
             reads=[ii[b], cv[b]], writes=[ii[b]])
        yield
        p.op("act", lambda e: e.activation(mm[b][:, :], mm[b][:, :], AF.Sqrt), reads=[mm[b]], writes=[mm[b]])
        p.op("act", lambda e: e.activation(gq[b][:, :], gs[:, :], AF.Square), reads=[gs], writes=[gq[b]])
        yield
        p.op("dve", lambda e: e.tensor_tensor(mm[b][:, :], mm[b][:, :], ii[b][:, :], ALU.mult),
             reads=[mm[b], ii[b]], writes=[mm[b]])
        yield
        p.op("dve", lambda e: e.tensor_tensor_scan(
            hs[b][:, :], aa[b][:, :], mm[b][:, :], carry[:, c:c + 1], ALU.mult, ALU.add),
            reads=[aa[b], mm[b], carry], writes=[hs[b]])
        p.op("pool", lambda e: e.tensor_copy(carry[:, c:c + 1], hs[b][:, 511:512]),
             reads=[hs[b]], writes=[carry])
        yield
        p.op("dve", lambda e: e.tensor_scalar(gq[b][:, :], gq[b][:, :], 0.044715, 1.0, ALU.mult, ALU.add),
             reads=[gq[b]], writes=[gq[b]])
        yield
        p.op("dve", lambda e: e.tensor_tensor(gq[b][:, :], gq[b][:, :], gs[:, :], ALU.mult),
             reads=[gq[b], gs], writes=[gq[b]])
        yield
        p.op("act", lambda e: e.activation(gq[b][:, :], gq[b][:, :], AF.Sigmoid, scale=1.5957691216057308),
             reads=[gq[b]], writes=[gq[b]])
        yield
        p.op("dve", lambda e: e.tensor_tensor(gq[b][:, :], gq[b][:, :], gs[:, :], ALU.mult),
             reads=[gq[b], gs], writes=[gq[b]])
        yield
        p.op("dve", lambda e: e.tensor_tensor(YT[:, c, :], hs[b][:, :], gq[b][:, :], ALU.mult),
             reads=[hs[b], gq[b]], writes=[YT])

    def interleave(gens, stagger=4, maxlive=4, mid_hook=None):
        pending = list(gens)
        n0 = len(pending)
        live = []
        rnd = 0
        last_start = -stagger
        while pending or live:
            if pending and len(live) < maxlive and rnd - last_start >= stagger:
                live.append(pending.pop(0))
                last_start = rnd
                if mid_hook is not None and n0 - len(pending) == 5:
                    mid_hook()
            for g in list(live):
                try:
                    next(g)
                except StopIteration:
                    live.remove(g)
            rnd += 1

    prep(0)
    for tt in range(NTT):
        interleave([chunk_gen(tt, c) for c in range(8)],
                   mid_hook=(lambda tt=tt: prep(tt + 1)) if tt + 1 < NTT else None)
        out_proj_residual(p, cx, YT, 8, Wo, h_in, h_out, tt * 512, rb, [psb[0], psb[1]], cnt)
    p.end_phase()


def ret_proj_phase(p, cx, layer, h_in, QT_d, KT_d, KZ_d, V2_d, SG_d):
    def load_cs(p, st, t):
        k = t % 2
        p.dma(st.cs[k][:, 0, :], cx.d_cos_d[:, t, :], writes=[st.cs[k]])
        p.dma(st.cs[k][:, 1, :], cx.d_sin_d[:, t, :], writes=[st.cs[k]])

    def setup(p, ps_tr):
        st = Ctx()
        st.cs = [p.sb("cs%d" % i, [128, 2, 256], F32) for i in range(2)]
        st.qk = [p.sb("qk%d" % i, [128, 2048], BF16) for i in range(2)]
        st.kz = [p.sb("kz%d" % i, [128, 1024], BF16) for i in range(2)]
        st.vb = [p.sb("vb%d" % i, [128, 2048], BF16) for i in range(2)]
        st.sg = [p.sb("sg%d" % i, [128, 2048], BF16) for i in range(2)]
        st.tmp = [p.sb("rt%d" % i, [128, 256], F32) for i in range(4)]
        st.QKT = [p.sb("QKT%d" % i, [128, 16, 512], BF16) for i in range(2)]
        st.ZT = p.sb("ZT", [128, 4], F32)
        p.dma(st.ZT[:, :], cx.d_zeta[:, :], writes=[st.ZT])
        load_cs(p, st, 0)
        return st

    def epi(p, st, t, gb, bank):
        k = t % 2
        if gb < 4:
            rope_bank(p, bank, st.qk[k], gb * 512, st.cs[k][:, 0, :], st.cs[k][:, 1, :], [st.cs[k]], 2, 256, 128, st.tmp)
            if gb >= 2:
                c0 = gb * 512
                p.op("act", lambda e: e.activation(st.qk[k][:, c0:c0 + 512], st.qk[k][:, c0:c0 + 512], AF.Copy,
                                                   scale=1.0 / 16),
                     reads=[st.qk[k]], writes=[st.qk[k]])
                for j in range(2):
                    hh = (gb - 2) * 2 + j
                    p.op("act", lambda e, j=j, hh=hh: e.activation(
                        st.kz[k][:, hh * 256:(hh + 1) * 256], st.qk[k][:, c0 + j * 256:c0 + (j + 1) * 256],
                        AF.Copy, scale=st.ZT[:, hh:hh + 1]),
                        reads=[st.qk[k], st.ZT], writes=[st.kz[k]])
        elif gb < 8:
            p.op("act", lambda e: e.activation(st.vb[k][:, (gb - 4) * 512:(gb - 3) * 512], bank[:, :], AF.Copy),
                 reads=[bank], writes=[st.vb[k]])
        else:
            p.op("act", lambda e: e.activation(st.sg[k][:, (gb - 8) * 512:(gb - 7) * 512], bank[:, :], AF.Silu),
                 reads=[bank], writes=[st.sg[k]])

    def tile_done(p, st, t, ps_tr):
        k = t % 2
        if t + 1 < NT:
            load_cs(p, st, t + 1)
        p.dma(KZ_d[t * 128:(t + 1) * 128, :], st.kz[k][:, :], reads=[st.kz[k]])
        p.dma(V2_d[t * 128:(t + 1) * 128, :], st.vb[k][:, :], reads=[st.vb[k]])
        p.dma(SG_d[t * 128:(t + 1) * 128, :], st.sg[k][:, :], reads=[st.sg[k]])
        g = (t // 4) % 2
        transpose_to(p, cx, st.qk[k], 16, ps_tr, st.QKT[g], (t % 4) * 128, evac_eng="act")
        if t % 4 == 3:
            t0 = (t // 4) * 512
            for hh in range(8):
                p.dma(QT_d[hh, :, t0:t0 + 512], st.QKT[g][:, hh, :], reads=[st.QKT[g]])
                p.dma(KT_d[hh, :, t0:t0 + 512], st.QKT[g][:, 8 + hh, :], reads=[st.QKT[g]])

    proj_phase(p, cx, layer, cx.d_w_in, 6144, h_in, setup, epi, tile_done)


def ret_core_phase(p, cx, h_in, h_out, QT_d, KT_d, KZ_d, V2_d, SG_d):
    p.begin_phase()
    load_consts(p, cx)
    psI = p.ps("psI", [128, 512], F32)
    psA = [p.ps("psA%d" % i, [128, 512], F32) for i in range(2)]
    psB = [p.ps("psB%d" % i, [128, 512], F32) for i in range(2)]
    psS = [p.ps("psS%d" % i, [128, 512], F32) for i in range(2)]
    ops1 = p.ps("ops1", [128, 512], F32)
    ops_b = ops1.h[:, :].bitcast(BF16)
    Wo = p.sb("Wo", [128, 16, D], BF16)
    stage = [p.sb("wst%d" % i, [128, 1024], F32) for i in range(2)]
    gnT = load_featmajor_vec(p, cx, cx.d_gn_g, 16, "gnT", psI)
    load_weight_bf16(p, cx.d_w_out, 2048, D, Wo, stage, gT=gnT, col_chunk=1024)
    DT = p.sb("DT", [128, 4, 128], F32)
    p.dma(DT[:, :, :], cx.d_decay[:, :, :], writes=[DT])
    XI = p.sb("XI", [128, 4], F32)
    p.dma(XI[:, :], cx.d_xi[:, :], writes=[XI])
    state_f = p.sb("state_f", [128, 8, 512], F32)
    state_b = p.sb("state_b", [128, 8, 512], BF16)
    p.op("pool", lambda e: e.memset(state_f[:, :, :], 0.0), writes=[state_f])
    p.op("pool", lambda e: e.memset(state_b[:, :, :], 0.0), writes=[state_b])
    QTs = [p.sb("QTs%d" % i, [128, 8, 512], BF16) for i in range(2)]
    KTs = [p.sb("KTs%d" % i, [128, 8, 512], BF16) for i in range(2)]
    kz = [p.sb("kz%d" % i, [128, 1024], BF16) for i in range(2)]
    vc = [p.sb("vc%d" % i, [128, 2048], BF16) for i in range(2)]
    sgc = [p.sb("sgc%d" % i, [128, 2048], BF16) for i in range(3)]
    inT = [p.sb("inT%d" % i, [128, 4, 128], BF16) for i in range(2)]
    oi = [p.sb("oi%d" % i, [128, 512], F32) for i in range(2)]
    oo = [p.sb("oo%d" % i, [128, 4, 512], F32) for i in range(2)]
    bst = [p.sb("bst%d" % i, [128, 4, 6], F32) for i in range(2)]
    mv = [p.sb("mv%d" % i, [128, 4, 2], F32) for i in range(2)]
    rsd = [p.sb("rsd%d" % i, [128, 4], F32) for i in range(2)]
    y = [p.sb("y%d" % i, [128, 2048], BF16) for i in range(2)]
    yT = [p.sb("yT%d" % i, [128, 16, 128], BF16) for i in range(2)]
    rb = [p.sb("rb%d" % i, [128, D], F32) for i in range(2)]
    gc = [float((1.0 - 2.0 ** (-5 - h)) ** 128) for h in range(4)]
    NC_ = NT if DBG_TILES is None else DBG_TILES

    def load_big(cq):
        for hh in range(8):
            p.dma(QTs[cq % 2][:, hh, :], QT_d[hh, :, cq * 512:(cq + 1) * 512], writes=[QTs[cq % 2]])
            p.dma(KTs[cq % 2][:, hh, :], KT_d[hh, :, cq * 512:(cq + 1) * 512], writes=[KTs[cq % 2]])

    def load_chunk(c):
        k = c % 2
        p.dma(kz[k][:, :], KZ_d[c * 128:(c + 1) * 128, :], writes=[kz[k]])
        p.dma(vc[k][:, :], V2_d[c * 128:(c + 1) * 128, :], writes=[vc[k]])
        p.dma(sgc[c % 3][:, :], SG_d[c * 128:(c + 1) * 128, :], writes=[sgc[c % 3]])

    def chunk(c):
        k = c % 2
        qt, kt_ = QTs[(c // 4) % 2], KTs[(c // 4) % 2]
        tc0 = (c % 4) * 128
        for h in range(4):
            for dk in range(2):
                p.op("pe", lambda e, h=h, dk=dk: e.matmul(
                    psI[:, h * 128:(h + 1) * 128], kt_[:, h * 2 + dk, tc0:tc0 + 128], qt[:, h * 2 + dk, tc0:tc0 + 128],
                    start=(dk == 0), stop=(dk == 1)), reads=[qt, kt_], writes=[psI])
        p.op("dve", lambda e: e.tensor_tensor(inT[k][:, :, :], psI[:, :].rearrange("p (h q) -> p h q", q=128),
                                              DT[:, :, :], ALU.mult), reads=[psI, DT], writes=[inT[k]])
        for h in range(4):
            b = h % 2
            for dk in range(2):
                p.op("pe", lambda e, h=h, dk=dk: e.matmul(
                    psS[dk][:, :], kz[k][:, (h * 2 + dk) * 128:(h * 2 + dk + 1) * 128], vc[k][:, h * 512:(h + 1) * 512],
                    start=True, stop=True), reads=[kz[k], vc[k]], writes=[psS[dk]])
            for dk in range(2):
                p.op("pe", lambda e, h=h, dk=dk, b=b: e.matmul(
                    psB[b][:, :], qt[:, h * 2 + dk, tc0:tc0 + 128], state_b[:, h * 2 + dk, :],
                    start=(dk == 0), stop=(dk == 1)), reads=[qt, state_b], writes=[psB[b]])
            p.op("pe", lambda e, h=h, b=b: e.matmul(
                psA[b][:, :], inT[k][:, h, :], vc[k][:, h * 512:(h + 1) * 512], start=True, stop=True),
                reads=[inT[k], vc[k]], writes=[psA[b]])
            for dk in range(2):
                p.op("dve", lambda e, h=h, dk=dk: e.scalar_tensor_tensor(
                    state_f[:, h * 2 + dk, :], state_f[:, h * 2 + dk, :], gc[h], psS[dk][:, :], ALU.mult, ALU.add),
                    reads=[state_f, psS[dk]], writes=[state_f])
            p.op("act", lambda e, b=b: e.activation(oi[b][:, :], psA[b][:, :], AF.Copy), reads=[psA[b]], writes=[oi[b]])
            for dk in range(2):
                p.op("act", lambda e, h=h, dk=dk: e.activation(state_b[:, h * 2 + dk, :], state_f[:, h * 2 + dk, :], AF.Copy),
                     reads=[state_f], writes=[state_b])
            p.op("dve", lambda e, h=h, b=b: e.scalar_tensor_tensor(
                oo[k][:, h, :], psB[b][:, :], XI[:, h:h + 1], oi[b][:, :], ALU.mult, ALU.add),
                reads=[psB[b], XI, oi[b]], writes=[oo[k]])
            p.op("dve", lambda e, h=h: e.bn_stats(bst[k][:, h, :], oo[k][:, h, :]), reads=[oo[k]], writes=[bst[k]])
            p.op("dve", lambda e, h=h: e.bn_aggr(mv[k][:, h, :], bst[k][:, h, :]), reads=[bst[k]], writes=[mv[k]])
    def chunk2(c):
        k = c % 2
        p.op("dve", lambda e: e.tensor_scalar(rsd[k][:, :], mv[k][:, :, 1], 1e-5, None, ALU.add),
             reads=[mv[k]], writes=[rsd[k]])
        p.op("act", lambda e: e.activation(rsd[k][:, :], rsd[k][:, :], AF.Sqrt), reads=[rsd[k]], writes=[rsd[k]])
        p.op("dve", lambda e: e.reciprocal(rsd[k][:, :], rsd[k][:, :]), reads=[rsd[k]], writes=[rsd[k]])
        for h in range(4):
            p.op("dve", lambda e, h=h: e.tensor_scalar(oo[k][:, h, :], oo[k][:, h, :], mv[k][:, h, 0:1], rsd[k][:, h:h + 1],
                                                      ALU.subtract, ALU.mult),
                 reads=[oo[k], mv[k], rsd[k]], writes=[oo[k]])
        oo2 = oo[k].h[:, :, :].rearrange("p h d -> p (h d)")
        p.op("dve", lambda e: e.tensor_tensor(y[k][:, :], oo2, sgc[c % 3][:, :], ALU.mult),
             reads=[oo[k], sgc[c % 3]], writes=[y[k]])
    def chunk2b(c):
        k = c % 2
        for j0 in range(0, 16, 8):
            for j in range(8):
                p.op("pe", lambda e, j=j, j0=j0: e.transpose(ops_b[:, j * 128:(j + 1) * 128],
                                                           y[k][:, (j0 + j) * 128:(j0 + j + 1) * 128], cx.ident_b[:, :]),
                     reads=[y[k], cx.ident_b_t], writes=[ops1])
            p.op("act", lambda e, j0=j0: e.activation(
                yT[k][:, j0:j0 + 8, :], ops_b[:, 0:1024].rearrange("p (j t) -> p j t", t=128), AF.Copy),
                reads=[ops1], writes=[yT[k]])
        hbt = rb[k]
        p.dma(hbt[:, :], h_in[c * 128:(c + 1) * 128, :], writes=[hbt])
        for half in range(2):
            for kc in range(16):
                p.op("pe", lambda e, kc=kc, half=half: e.matmul(
                    ops1[:, :], yT[k][:, kc, :], Wo[:, kc, half * 512:(half + 1) * 512],
                    start=(kc == 0), stop=(kc == 15)), reads=[yT[k], Wo], writes=[ops1])
            p.op("dve", lambda e, half=half: e.tensor_tensor(
                hbt[:, half * 512:(half + 1) * 512], hbt[:, half * 512:(half + 1) * 512], ops1[:, :], ALU.add),
                reads=[hbt, ops1], writes=[hbt])
        p.dma(h_out[c * 128:(c + 1) * 128, :], hbt[:, :], reads=[hbt])

    load_big(0)
    load_chunk(0)
    for c in range(NC_):
        if c % 4 == 0 and (c // 4 + 1) * 4 < NC_:
            load_big(c // 4 + 1)
        if c + 1 < NC_:
            load_chunk(c + 1)
        if c >= 1:
            chunk2(c - 1)
        chunk(c)
        if c >= 1:
            chunk2b(c - 1)
    chunk2(NC_ - 1)
    chunk2b(NC_ - 1)
    p.end_phase()


def final_phase(p, cx, h_in, out):
    p.begin_phase()
    gb = p.sb("gb", [128, D], F32)
    p.dma(gb[:, :], cx.norm_final_g.rearrange("(o d) -> o d", o=1).broadcast_to([128, D]), writes=[gb])
    hb = [p.sb("hb%d" % i, [128, D], F32) for i in range(3)]
    ob = [p.sb("ob%d" % i, [128, D], F32) for i in range(2)]
    junk = p.sb("junk", [128, D], BF16)
    ss = [p.sb("ss%d" % i, [128, 1], F32) for i in range(2)]
    rstd = [p.sb("rstd%d" % i, [128, 1], F32) for i in range(2)]
    for t in range(NT):
        hbt = hb[t % 3]
        k = t % 2
        p.dma(hbt[:, :], h_in[t * 128:(t + 1) * 128, :], writes=[hbt])
        p.op("act", lambda e, hbt=hbt, k=k: e.activation(junk[:, :], hbt[:, :], AF.Square, accum_out=ss[k][:, 0:1]),
             reads=[hbt], writes=[junk, ss[k]])
        p.op("dve", lambda e, k=k: e.tensor_scalar(rstd[k][:, 0:1], ss[k][:, 0:1], 1.0 / D, RMS_EPS, ALU.mult, ALU.add),
             reads=[ss[k]], writes=[rstd[k]])
        p.op("act", lambda e, k=k: e.activation(rstd[k][:, 0:1], rstd[k][:, 0:1], AF.Sqrt), reads=[rstd[k]], writes=[rstd[k]])
        p.op("dve", lambda e, k=k: e.reciprocal(rstd[k][:, 0:1], rstd[k][:, 0:1]), reads=[rstd[k]], writes=[rstd[k]])
        p.op("dve", lambda e, hbt=hbt, k=k: e.scalar_tensor_tensor(
            ob[k][:, :], hbt[:, :], rstd[k][:, 0:1], gb[:, :], ALU.mult, ALU.mult),
            reads=[hbt, rstd[k], gb], writes=[ob[k]])
        p.dma(out[t * 128:(t + 1) * 128, :], ob[k][:, :], reads=[ob[k]])
    p.end_phase()


INPUT_SPECS = [
    ("x", [S, D]), ("norm_mix_g", [4, D]), ("norm_ffn_g", [4, D]), ("norm_final_g", [D]),
    ("a_w_in", [D, 3072]), ("a_w_out", [D, D]),
    ("b_w_in", [D, 3072]), ("b_w_out", [D, D]), ("b_lam_q1", [64]), ("b_lam_k1", [64]),
    ("b_lam_q2", [64]), ("b_lam_k2", [64]), ("b_subln_g", [128]),
    ("c_w_in", [D, 2048]), ("c_conv_w", [4, D]), ("c_conv_b", [D]), ("c_w_a", [8, 128, 128]), ("c_b_a", [D]),
    ("c_w_x", [8, 128, 128]), ("c_b_x", [D]), ("c_lambda", [D]), ("c_w_out", [D, D]),
    ("d_w_in", [D, 6144]), ("d_gn_g", [2048]), ("d_w_out", [2048, D]),
    ("ffn_w_in", [4, D, 2 * D_FF]), ("ffn_conv_w", [4, 3, D_FF]), ("ffn_conv_b", [4, D_FF]),
    ("ffn_w_out", [4, D_FF, D]),
]


def rope_np(half_dim_total, theta, rep):
    inv = theta ** (-np.arange(0, half_dim_total, 2, dtype=np.float32) / np.float32(half_dim_total))
    ang = np.arange(S, dtype=np.float32)[:, None] * inv[None, :].astype(np.float32)
    lay = lambda a: np.ascontiguousarray(
        np.tile(a.astype(np.float32), (1, rep)).reshape(NT, 128, -1).transpose(1, 0, 2))
    return lay(np.cos(ang)), lay(np.sin(ang))


def host_consts():
    bf = ml_dtypes.bfloat16
    c = {}
    c["k_ident_f"] = np.eye(128, dtype=np.float32)
    c["k_ident_b"] = np.eye(128, dtype=np.float32).astype(bf)
    c["k_cos_a"], c["k_sin_a"] = rope_np(32, 500000.0, 4)
    c["k_cos_b"], c["k_sin_b"] = rope_np(16, 500000.0, 8)
    c["k_cos_d"], c["k_sin_d"] = rope_np(256, 10000.0, 2)
    kk = np.arange(128)[:, None, None] + 128 * np.arange(4)[None, :, None]
    qq = np.arange(512)[None, None, :]
    c["k_cmask"] = np.where(kk <= qq, 0.0, NEG).astype(np.float32).astype(bf)
    en = np.zeros((16, 16, 128), np.float32)
    for n in range(16):
        en[n, n, :] = 1.0
    c["k_en"] = en.astype(bf)
    qb = np.arange(16)[:, None]
    n = np.arange(16)[None, :]
    pb2 = np.where(n < qb, 0.0, np.where(n == qb, 1e30, -2e30)).astype(np.float32)
    c["k_pb2"] = np.ascontiguousarray(np.broadcast_to(pb2[None], (128, 16, 16)))
    lg = np.log1p(-np.exp2(-5.0 - np.arange(4, dtype=np.float64)))
    pos = np.arange(128, dtype=np.float64)
    kk_ = pos[:, None, None]
    qq_ = pos[None, None, :]
    dec = np.where(qq_ >= kk_, np.exp(np.maximum(qq_ - kk_, 0.0) * lg[None, :, None]), 0.0)
    c["k_decay"] = dec.astype(np.float32)
    c["k_xi"] = np.exp((pos[:, None] + 1.0) * lg[None, :]).astype(np.float32)
    c["k_zeta"] = np.exp((127.0 - pos[:, None]) * lg[None, :]).astype(np.float32)
    return c


CONST_SPECS = [("k_ident_f", [128, 128], F32), ("k_ident_b", [128, 128], BF16),
               ("k_cos_a", [128, NT, 64], F32), ("k_sin_a", [128, NT, 64], F32),
               ("k_cos_b", [128, NT, 64], F32), ("k_sin_b", [128, NT, 64], F32),
               ("k_cos_d", [128, NT, 256], F32), ("k_sin_d", [128, NT, 256], F32),
               ("k_cmask", [128, 4, 512], BF16), ("k_en", [16, 16, 128], BF16), ("k_pb2", [128, 16, 16], F32),
               ("k_decay", [128, 4, 128], F32), ("k_xi", [128, 4], F32), ("k_zeta", [128, 4], F32)]


def build_program(stages):
    nc = bass.Bass("TRN2", target_bir_lowering=False)
    cx = Ctx()
    for name, shape in INPUT_SPECS:
        setattr(cx, name, nc.dram_tensor(name, list(shape), F32, kind="ExternalInput").ap())
    for name, shape, dt in CONST_SPECS:
        setattr(cx, "d_" + name[2:], nc.dram_tensor(name, list(shape), dt, kind="ExternalInput").ap())
    out = nc.dram_tensor("out", [S, D], F32, kind="ExternalOutput").ap()
    hA = nc.dram_tensor("hA", [S, D], F32).ap()
    QT_d = nc.dram_tensor("QT_d", [8, 128, S], BF16).ap()
    KT_d = nc.dram_tensor("KT_d", [8, 128, S], BF16).ap()
    V_d = nc.dram_tensor("V_d", [S, D], BF16).ap()
    V2_d = nc.dram_tensor("V2_d", [S, 2048], BF16).ap()
    KM_d = nc.dram_tensor("KM_d", [128, 8, 16], F32).ap()
    SG_d = nc.dram_tensor("SG_d", [S, 2048], BF16).ap()
    p = Prog(nc)
    cur = cx.x
    for st in stages:
        kind = st[0]
        if kind == "ffn":
            ffn_phase(p, cx, st[1], cur, hA)
            cur = hA
        elif kind == "diff":
            qkv_proj_phase(p, cx, 1, cx.b_w_in, cur, cx.d_cos_b, cx.d_sin_b, 8, 64, 8, QT_d, KT_d, V_d)
            diff_attn_phase(p, cx, cur, hA, QT_d, KT_d, V_d)
            cur = hA
        elif kind == "rglru":
            rglru_phase(p, cx, 2, cur, hA)
            cur = hA
        elif kind == "ret":
            ret_proj_phase(p, cx, 3, cur, QT_d, KT_d, V_d, V2_d, SG_d)
            ret_core_phase(p, cx, cur, hA, QT_d, KT_d, V_d, V2_d, SG_d)
            cur = hA
        elif kind == "moba_a":
            qkv_proj_phase(p, cx, 0, cx.a_w_in, cur, cx.d_cos_a, cx.d_sin_a, 4, 128, 16, QT_d, KT_d, V_d)
        elif kind == "moba":
            qkv_proj_phase(p, cx, 0, cx.a_w_in, cur, cx.d_cos_a, cx.d_sin_a, 4, 128, 16, QT_d, KT_d, V_d, KM_d=KM_d)
            moba_attn_phase(p, cx, cur, hA, QT_d, KT_d, V_d, KM_d)
            cur = hA
        elif kind == "final":
            final_phase(p, cx, cur, out)
        elif kind == "copyout":
            copy_phase(p, cx, cur, out)
    p.es.close()
    return nc


def copy_phase(p, cx, h_in, out):
    p.begin_phase()
    hb = [p.sb("hb%d" % i, [128, D], F32) for i in range(4)]
    for t in range(NT):
        hbt = hb[t % 4]
        p.dma(hbt[:, :], h_in[t * 128:(t + 1) * 128, :], writes=[hbt])
        p.dma(out[t * 128:(t + 1) * 128, :], hbt[:, :], reads=[hbt])
    p.end_phase()


FULL_STAGES = [("moba",), ("ffn", 0), ("diff",), ("ffn", 1), ("rglru",), ("ffn", 2), ("ret",), ("ffn", 3), ("final",)]


def run(inputs, stages, trace=False):
    nc = build_program(stages)
    consts = host_consts()
    in_maps = []
    for b in range(8):
        m = {}
        for name, shape in INPUT_SPECS:
            a = np.asarray(inputs[name])
            if name == "x":
                a = a[b]
            elif name.startswith(("a_", "b_", "c_", "d_")):
                a = a[0]
            m[name] = np.ascontiguousarray(a.reshape(shape).astype(np.float32, copy=False))
        m.update(consts)
        in_maps.append(m)
    res = run_bass_kernel_spmd(nc, in_maps, core_ids=list(range(8)), trace=trace)
    return np.stack([np.asarray(r["out"]) for r in res.results], axis=0), res


def kernel(**inputs):
    out, _ = run(inputs, FULL_STAGES)
    return out.astype(np.float32)
```
